# Optimizing a Trainium2 kernel written in Bass

```python
import jax
import jax.numpy as jnp
from jax import lax
import numpy as np

D_MODEL = 2048
BATCH = 32
SEQ = 256
DEPTH = 2
DEC_BATCH = 8
DEC_SEQ = 4096
PAST_LEN = 512

GRID_W = 64
N_DIR = 2
N_BRANCH = 3
BRANCH_W = 1024
GLA_HEADS = 4
GLA_DK = 128
GLA_DV = 256
GLA_RANK = 16
GLA_TAU = 16.0
GLA_CHUNK = 64
MLSTM_HEADS = 4
MLSTM_DH = 256
MLSTM_CHUNK = 64
LRU_WIDTH = 1024
LRU_BLOCKS = 8
LRU_BLOCK = 128
LRU_CONV = 4
LRU_C = 8.0
D_FF = 5632
FFN_CONV = 3
EPS = 1e-6
IN_SPLITS = (
    GLA_HEADS * GLA_DK, GLA_HEADS * GLA_DK, GLA_HEADS * GLA_DV, GLA_HEADS * GLA_DV, N_DIR * GLA_RANK,
    MLSTM_HEADS * MLSTM_DH, MLSTM_HEADS * MLSTM_DH, MLSTM_HEADS * MLSTM_DH, MLSTM_HEADS * MLSTM_DH, N_DIR * 2 * MLSTM_HEADS,
    LRU_WIDTH, LRU_WIDTH,
    N_BRANCH * D_MODEL,
)
D_IN = sum(IN_SPLITS)

kernel_name = 'bidir_gla_mlstm_rglru_diffusion_step'


def _flip(t):
    return jnp.flip(t, axis=1)


def _chunks(t, size):
    b, n = t.shape[0], t.shape[1]
    return jnp.moveaxis(t.reshape((b, n // size, size) + t.shape[2:]), 1, 0)


def _unchunks(t):
    n, b, size = t.shape[0], t.shape[1], t.shape[2]
    return jnp.moveaxis(t, 0, 1).reshape((b, n * size) + t.shape[3:])


def rmsnorm(x, g):
    xf = x.astype(jnp.float32)
    y = xf * lax.rsqrt(jnp.mean(xf * xf, axis=-1, keepdims=True) + EPS)
    return (y * g.astype(jnp.float32)).astype(x.dtype)


def head_rmsnorm(x, g):
    b, t, h, d = x.shape
    y = x * lax.rsqrt(jnp.mean(x * x, axis=-1, keepdims=True) + EPS)
    return y.reshape(b, t, h * d) * g.astype(jnp.float32)


def dwconv1d(x, w, bias, left):
    k, t = w.shape[0], x.shape[1]
    xp = jnp.pad(x, ((0, 0), (left, k - 1 - left), (0, 0)))
    return sum(xp[:, j:j + t] * w[j] for j in range(k)) + bias


def dwconv_grid(x, w, bias):
    b, t, ch = x.shape
    rows = t // GRID_W
    p = FFN_CONV // 2
    xg = jnp.pad(x.reshape(b, rows, GRID_W, ch), ((0, 0), (p, p), (p, p), (0, 0)))
    y = sum(xg[:, dy:dy + rows, dx:dx + GRID_W] * w[dy, dx] for dy in range(FFN_CONV) for dx in range(FFN_CONV))
    return (y + bias).reshape(b, t, ch)


def gla_scan(q, k, v, log_a, s0):
    causal = jnp.tril(jnp.ones((GLA_CHUNK, GLA_CHUNK), dtype=bool))

    def step(s, inp):
        qc, kc, vc, lc = inp
        cum = jnp.cumsum(lc, axis=1)
        last = cum[:, -1]
        q_in = qc * jnp.exp(cum)
        k_in = kc * jnp.exp(-cum)
        att = jnp.where(causal, jnp.einsum('bihd,bjhd->bhij', q_in, k_in), 0.0)
        o = jnp.einsum('bhij,bjhv->bihv', att, vc) + jnp.einsum('bihd,bhdv->bihv', q_in, s)
        k_end = kc * jnp.exp(last[:, None] - cum)
        s = jnp.exp(last)[..., None] * s + jnp.einsum('bjhd,bjhv->bhdv', k_end, vc)
        return s, o

    s_end, o = lax.scan(step, s0.astype(jnp.float32), tuple(_chunks(t, GLA_CHUNK) for t in (q, k, v, log_a)))
    return _unchunks(o), s_end


def mlstm_scan(q, k, v, log_f, log_i, c0, n0, m0):
    causal = jnp.tril(jnp.ones((MLSTM_CHUNK, MLSTM_CHUNK), dtype=bool))

    def step(carry, inp):
        cmat, nvec, m = carry
        qc, kc, vc, fc, ic = inp
        fcum = jnp.moveaxis(jnp.cumsum(fc, axis=1), 1, -1)
        ig = jnp.moveaxis(ic, 1, -1)
        dlog = jnp.where(causal, fcum[..., :, None] - fcum[..., None, :] + ig[..., None, :], -jnp.inf)
        prev = fcum + m[..., None]
        mj = jnp.maximum(prev, jnp.max(dlog, axis=-1))
        w = jnp.exp(dlog - mj[..., None])
        wp = jnp.moveaxis(jnp.exp(prev - mj), -1, 1)[..., None]
        s = jnp.einsum('bjhd,bihd->bhji', qc, kc) * w
        qp = qc * wp
        num = jnp.einsum('bhji,bihv->bjhv', s, vc) + jnp.einsum('bjhd,bhdv->bjhv', qp, cmat)
        den = jnp.moveaxis(jnp.sum(s, axis=-1), -1, 1) + jnp.einsum('bjhd,bhd->bjh', qp, nvec)
        floor = jnp.moveaxis(jnp.exp(-mj), -1, 1)
        h = num / jnp.maximum(jnp.abs(den), floor)[..., None]
        m_new = mj[..., -1]
        wl = jnp.exp(fcum[..., -1:] - fcum + ig - m_new[..., None])
        decay = jnp.exp(fcum[..., -1] + m - m_new)
        kw = kc * jnp.moveaxis(wl, -1, 1)[..., None]
        cmat = decay[..., None, None] * cmat + jnp.einsum('bihd,bihv->bhdv', kw, vc)
        nvec = decay[..., None] * nvec + jnp.sum(kw, axis=1)
        return (cmat, nvec, m_new), h

    f32 = jnp.float32
    init = (c0.astype(f32), n0.astype(f32), m0.astype(f32))
    (c_end, n_end, m_end), h = lax.scan(step, init, tuple(_chunks(t, MLSTM_CHUNK) for t in (q, k, v, log_f, log_i)))
    return _unchunks(h), c_end, n_end, m_end


def rglru_dir(x, gate_w, gate_b, lam, h0):
    b, t, w = x.shape
    f32 = jnp.float32
    xb = x.reshape(b, t, LRU_BLOCKS, LRU_BLOCK)
    pre = jnp.einsum('btnk,gnkj->gbtnj', xb, gate_w.astype(f32)).reshape(2, b, t, w)
    gates = jax.nn.sigmoid(pre + gate_b.astype(f32)[:, None, None, :])
    log_a = -LRU_C * gates[0] * jax.nn.softplus(-lam.astype(f32))
    a = jnp.exp(log_a)
    inp = jnp.sqrt(-jnp.expm1(2.0 * log_a)) * (gates[1] * x)
    inp = inp.at[:, 0].add(a[:, 0] * h0.astype(f32))

    def combine(left, right):
        return left[0] * right[0], right[0] * left[1] + right[1]

    _, h = lax.associative_scan(combine, (a, inp), axis=1)
    return h, h[:, -1]


def token_mix(u, init, w_in, gla_w_alpha, gla_b_alpha, gla_norm_g, mlstm_b_if, mlstm_norm_g,
              lru_conv_w, lru_conv_b, lru_gate_w, lru_gate_b, lru_lambda, w_branch, b_merge, w_out):
    b, t, _ = u.shape
    f32 = jnp.float32
    s_gla0, c_ml0, n_ml0, m_ml0, h_lru0 = init
    z = u @ w_in
    (qa, ka, va, ga, ra, qb, kb, vb, ob, gb, xr, yr, mg) = jnp.split(z, np.cumsum(IN_SPLITS)[:-1].tolist(), axis=-1)

    qa = qa.reshape(b, t, GLA_HEADS, GLA_DK).astype(f32) * GLA_DK ** -0.5
    ka = ka.reshape(b, t, GLA_HEADS, GLA_DK).astype(f32)
    va = va.reshape(b, t, GLA_HEADS, GLA_DV).astype(f32)
    la = jax.nn.log_sigmoid(jnp.einsum('btnr,nrk->btnk', ra.reshape(b, t, N_DIR, GLA_RANK).astype(f32),
                                       gla_w_alpha.astype(f32)) + gla_b_alpha.astype(f32)) / GLA_TAU
    la = la.reshape(b, t, N_DIR, GLA_HEADS, GLA_DK)
    oa_f, sa_f = gla_scan(qa, ka, va, la[:, :, 0], s_gla0[:, 0])
    oa_r, sa_r = gla_scan(_flip(qa), _flip(ka), _flip(va), _flip(la[:, :, 1]), s_gla0[:, 1])
    y_a = head_rmsnorm(oa_f + _flip(oa_r), gla_norm_g) * jax.nn.silu(ga.astype(f32))

    qb = qb.reshape(b, t, MLSTM_HEADS, MLSTM_DH).astype(f32)
    kb = kb.reshape(b, t, MLSTM_HEADS, MLSTM_DH).astype(f32) * MLSTM_DH ** -0.5
    vb = vb.reshape(b, t, MLSTM_HEADS, MLSTM_DH).astype(f32)
    gif = gb.reshape(b, t, N_DIR, 2, MLSTM_HEADS).astype(f32) + mlstm_b_if.astype(f32)
    log_i = gif[:, :, :, 0]
    log_f = jax.nn.log_sigmoid(gif[:, :, :, 1])
    hb_f, cb_f, nb_f, mb_f = mlstm_scan(qb, kb, vb, log_f[:, :, 0], log_i[:, :, 0], c_ml0[:, 0], n_ml0[:, 0], m_ml0[:, 0])
    hb_r, cb_r, nb_r, mb_r = mlstm_scan(_flip(qb), _flip(kb), _flip(vb), _flip(log_f[:, :, 1]), _flip(log_i[:, :, 1]),
                                        c_ml0[:, 1], n_ml0[:, 1], m_ml0[:, 1])
    y_b = jax.nn.sigmoid(ob.astype(f32)) * head_rmsnorm(hb_f + _flip(hb_r), mlstm_norm_g)

    xr = dwconv1d(xr.astype(f32), lru_conv_w.astype(f32), lru_conv_b.astype(f32), LRU_CONV // 2)
    hc_f, hc_f_end = rglru_dir(xr, lru_gate_w[0], lru_gate_b[0], lru_lambda[0], h_lru0[:, 0])
    hc_r, hc_r_end = rglru_dir(_flip(xr), lru_gate_w[1], lru_gate_b[1], lru_lambda[1], h_lru0[:, 1])
    y_c = (hc_f + _flip(hc_r)) * jax.nn.gelu(yr.astype(f32))

    g = jax.nn.sigmoid(mg.reshape(b, t, N_BRANCH, D_MODEL).astype(f32) + b_merge.reshape(N_BRANCH, D_MODEL).astype(f32))
    merged = sum(g[:, :, n] * (y @ w_branch[n]) for n, y in enumerate((y_a, y_b, y_c)))
    out = merged.astype(u.dtype) @ w_out
    new_state = (jnp.stack([sa_f, sa_r], axis=1), jnp.stack([cb_f, cb_r], axis=1), jnp.stack([nb_f, nb_r], axis=1),
                 jnp.stack([mb_f, mb_r], axis=1), jnp.stack([hc_f_end, hc_r_end], axis=1))
    return out, new_state


def conv_ffn(v, w_up, conv_w, conv_b, w_down, on_grid):
    hg, hu = jnp.split(v @ w_up, 2, axis=-1)
    if on_grid:
        hg = dwconv_grid(hg, conv_w, conv_b)
    else:
        hg = dwconv1d(hg, conv_w[FFN_CONV // 2], conv_b, FFN_CONV // 2)
    return (jax.nn.silu(hg) * hu) @ w_down


def block(x, mod, on_grid, init, lp):
    (n1, n2, w_in, gwa, gba, gng, mbif, mng, lcw, lcb, lgw, lgb, llam, wbr, bmg, wout, fup, fcw, fcb, fdown) = lp
    sh1, sc1, g1, sh2, sc2, g2 = jnp.split(mod, 6, axis=-1)
    u = rmsnorm(x, n1) * (1.0 + sc1) + sh1
    mix, new_state = token_mix(u, init, w_in, gwa, gba, gng, mbif, mng, lcw, lcb, lgw, lgb, llam, wbr, bmg, wout)
    x = x + g1 * mix
    v = rmsnorm(x, n2) * (1.0 + sc2) + sh2
    x = x + g2 * conv_ffn(v, fup, fcw, fcb, fdown, on_grid)
    return x, new_state


def zero_context_state(b):
    f32 = jnp.float32
    return (jnp.zeros((b, N_DIR, GLA_HEADS, GLA_DK, GLA_DV), f32),
            jnp.zeros((b, N_DIR, MLSTM_HEADS, MLSTM_DH, MLSTM_DH), f32),
            jnp.zeros((b, N_DIR, MLSTM_HEADS, MLSTM_DH), f32),
            jnp.zeros((b, N_DIR, MLSTM_HEADS), f32),
            jnp.zeros((b, N_DIR, LRU_WIDTH), f32))


def setup_inputs(seed: int = 0) -> dict:
    key = jax.random.key(seed)
    keys = iter(jax.random.split(key, 40))
    f32 = jnp.float32
    d = D_MODEL

    def nrm(shape, scale):
        return jax.random.normal(next(keys), shape, f32) * scale

    a_c = jax.random.uniform(next(keys), (DEPTH, N_DIR, LRU_WIDTH), f32, 0.9, 0.999)
    a = a_c ** (1.0 / LRU_C)
    lru_lambda = jnp.log(a) - jnp.log1p(-a)
    return {
        'x_prompt': nrm((BATCH, SEQ, d), 1.0),
        'x_sample': nrm((DEC_BATCH, DEC_SEQ, d), 1.0),
        'state_gla': nrm((DEC_BATCH, DEPTH, N_DIR, GLA_HEADS, GLA_DK, GLA_DV), 1.0),
        'state_mlstm_c': nrm((DEC_BATCH, DEPTH, N_DIR, MLSTM_HEADS, MLSTM_DH, MLSTM_DH), 0.1),
        'state_mlstm_n': nrm((DEC_BATCH, DEPTH, N_DIR, MLSTM_HEADS, MLSTM_DH), 0.2),
        'state_mlstm_m': nrm((DEC_BATCH, DEPTH, N_DIR, MLSTM_HEADS), 1.0),
        'state_rglru': nrm((DEC_BATCH, DEPTH, N_DIR, LRU_WIDTH), 0.5),
        'c': nrm((DEC_BATCH, d), 1.0),
        'c_ctx': nrm((d,), 1.0),
        'norm1_g': 1.0 + nrm((DEPTH, d), 0.02),
        'norm2_g': 1.0 + nrm((DEPTH, d), 0.02),
        'w_mod': nrm((DEPTH, d, 6 * d), 0.5 * d ** -0.5),
        'b_mod': nrm((DEPTH, 6 * d), 0.01),
        'w_in': nrm((DEPTH, d, D_IN), d ** -0.5),
        'gla_w_alpha': nrm((DEPTH, N_DIR, GLA_RANK, GLA_HEADS * GLA_DK), GLA_RANK ** -0.5),
        'gla_b_alpha': nrm((DEPTH, N_DIR, GLA_HEADS * GLA_DK), 0.1),
        'gla_norm_g': 1.0 + nrm((DEPTH, GLA_HEADS * GLA_DV), 0.02),
        'mlstm_b_if': nrm((DEPTH, N_DIR, 2, MLSTM_HEADS), 0.1) + jnp.array([0.0, 3.0], f32)[:, None],
        'mlstm_norm_g': 1.0 + nrm((DEPTH, MLSTM_HEADS * MLSTM_DH), 0.02),
        'lru_conv_w': nrm((DEPTH, LRU_CONV, LRU_WIDTH), LRU_CONV ** -0.5),
        'lru_conv_b': nrm((DEPTH, LRU_WIDTH), 0.01),
        'lru_gate_w': nrm((DEPTH, N_DIR, 2, LRU_BLOCKS, LRU_BLOCK, LRU_BLOCK), LRU_BLOCK ** -0.5),
        'lru_gate_b': nrm((DEPTH, N_DIR, 2, LRU_WIDTH), 0.01),
        'lru_lambda': lru_lambda,
        'w_branch': nrm((DEPTH, N_BRANCH, BRANCH_W, d), BRANCH_W ** -0.5),
        'b_merge': nrm((DEPTH, N_BRANCH * d), 0.01),
        'w_out': nrm((DEPTH, d, d), d ** -0.5),
        'ffn_w_up': nrm((DEPTH, d, 2 * D_FF), d ** -0.5),
        'ffn_conv_w': nrm((DEPTH, FFN_CONV, FFN_CONV, D_FF), 1.0 / FFN_CONV),
        'ffn_conv_b': nrm((DEPTH, D_FF), 0.01),
        'ffn_w_down': nrm((DEPTH, D_FF, d), D_FF ** -0.5),
        'norm_f_g': 1.0 + nrm((d,), 0.02),
    }


def reference(x_prompt, x_sample, state_gla, state_mlstm_c, state_mlstm_n, state_mlstm_m, state_rglru, c, c_ctx,
              norm1_g, norm2_g, w_mod, b_mod, w_in, gla_w_alpha, gla_b_alpha, gla_norm_g, mlstm_b_if, mlstm_norm_g,
              lru_conv_w, lru_conv_b, lru_gate_w, lru_gate_b, lru_lambda, w_branch, b_merge, w_out,
              ffn_w_up, ffn_conv_w, ffn_conv_b, ffn_w_down, norm_f_g):
    xp, xs = x_prompt, x_sample
    ctx_init = zero_context_state(x_prompt.shape[0])
    new_gla, new_mc, new_mn, new_mm, new_lru = [], [], [], [], []
    for l in range(DEPTH):
        lp = (norm1_g[l], norm2_g[l], w_in[l], gla_w_alpha[l], gla_b_alpha[l], gla_norm_g[l], mlstm_b_if[l],
              mlstm_norm_g[l], lru_conv_w[l], lru_conv_b[l], lru_gate_w[l], lru_gate_b[l], lru_lambda[l],
              w_branch[l], b_merge[l], w_out[l], ffn_w_up[l], ffn_conv_w[l], ffn_conv_b[l], ffn_w_down[l])
        mod_ctx = (jax.nn.silu(c_ctx) @ w_mod[l] + b_mod[l])[None, None, :]
        mod_lat = (jax.nn.silu(c) @ w_mod[l] + b_mod[l])[:, None, :]
        xp, st = block(xp, mod_ctx, False, ctx_init, lp)
        new_gla.append(st[0])
        new_mc.append(st[1])
        new_mn.append(st[2])
        new_mm.append(st[3])
        new_lru.append(st[4])
        lat_init = (state_gla[:, l], state_mlstm_c[:, l], state_mlstm_n[:, l], state_mlstm_m[:, l], state_rglru[:, l])
        xs, _ = block(xs, mod_lat, True, lat_init, lp)
    y_prompt = rmsnorm(xp, norm_f_g)
    y_sample = rmsnorm(xs, norm_f_g)
    new_state_gla = jnp.stack(new_gla, axis=1)
    new_state_mlstm_c = jnp.stack(new_mc, axis=1)
    new_state_mlstm_n = jnp.stack(new_mn, axis=1)
    new_state_mlstm_m = jnp.stack(new_mm, axis=1)
    new_state_rglru = jnp.stack(new_lru, axis=1)
    return (y_prompt, y_sample, new_state_gla, new_state_mlstm_c, new_state_mlstm_n, new_state_mlstm_m, new_state_rglru)
```

```python
import os
import math
from contextlib import ExitStack
import numpy as np
import concourse.bass as bass
import concourse.mybir as mybir
from concourse.bass_utils import run_bass_kernel_spmd

F32 = mybir.dt.float32
BF16 = mybir.dt.bfloat16
ALU = mybir.AluOpType
AF = mybir.ActivationFunctionType
AX = mybir.AxisListType

NCORES = 8
DM = 2048
T = 5120
LAT = 4096
NCTX = 4
CTXL = 256
DEPTH = 2
D_IN = 15408
D_FF = 5632
EPS = 1e-6
NB = 256
LN_QA = math.log(128 ** -0.5)
LN_KB = math.log(256 ** -0.5)

C_QA, C_KA, C_VA, C_GA, C_RA, C_QB, C_KB, C_VB, C_OB, C_GB, C_XR, C_YR, C_MG = (
    0, 512, 1024, 2048, 3072, 3104, 4128, 5152, 6176, 7200, 7216, 8240, 9264)

STOP = os.environ.get("MK_STOP", "")
STOPL = int(os.environ.get("MK_STOPL", "0"))
SKIP = os.environ.get("MK_SKIP", "")
DUMP = [s for s in os.environ.get("MK_DUMP", "").split(",") if s]


class Buf:
    __slots__ = ("w", "r")

    def __init__(self):
        self.w = None
        self.r = {}


class Tl:
    def __init__(self, h):
        self.h = h
        self.buf = Buf()

    def __getitem__(self, k):
        return self.h[k]


def _b(x):
    return getattr(x, "buf", x)


class Prog:
    def __init__(self, nc, es):
        self.nc = nc
        self.eng = {"pe": nc.tensor, "dve": nc.vector, "act": nc.scalar, "sp": nc.sync}
        self.sem = {k: es.enter_context(nc.semaphore("s_" + k)) for k in ("pe", "dve", "act")}
        self.cnt = {"pe": 0, "dve": 0, "act": 0}
        self.known = {k: {} for k in self.eng}
        self.slots = {}
        for q, n in (("sp", 40), ("act", 40)):
            self.slots[q] = [[es.enter_context(nc.semaphore(f"d_{q}{i}")), 0, f"{q}{i}"] for i in range(n)]
        self.slot_i = {"sp": 0, "act": 0}
        self.pe_pending = False
        self.psum = es.enter_context(nc.psum_tensor("ps", [128, 4096], F32))
        self.pbuf = [Buf() for _ in range(8)]
        self.pi = 0
        self.nins = 0

    def _wait(self, e, tok):
        key, sem, val = tok
        if self.known[e].get(key, 0) >= val:
            return
        self.eng[e].wait_ge(sem, val)
        self.known[e][key] = val
        self.nins += 1

    def _deps(self, e, r, w):
        toks = []
        for b in r:
            if b.w is not None:
                toks.append(b.w)
        for b in w:
            if b.w is not None and b.w[0] != e:
                toks.append(b.w)
            toks.extend(b.r.values())
        for t in toks:
            if e == "pe" and t[0] == "pe":
                continue
            self._wait(e, t)

    def _mark(self, tok, r, w):
        for b in r:
            old = b.r.get(tok[0])
            if old is None or old[2] < tok[2]:
                b.r[tok[0]] = tok
        for b in w:
            b.w = tok
            b.r = {}

    def op(self, e, fn, r=(), w=(), ms=True):
        r = [_b(x) for x in r]
        w = [_b(x) for x in w]
        self._deps(e, r, w)
        ins = fn(self.eng[e])
        self.nins += 1
        if e == "pe":
            tok = ("pe", self.sem["pe"], self.cnt["pe"] + 1)
            if ms:
                ins.then_inc(self.sem["pe"], 1)
                self.cnt["pe"] += 1
                self.pe_pending = False
            else:
                self.pe_pending = True
        else:
            self.cnt[e] += 1
            ins.then_inc(self.sem[e], 1)
            tok = (e, self.sem[e], self.cnt[e])
        self._mark(tok, r, w)

    def pe(self, fn, r=(), w=(), ms=True):
        self.op("pe", fn, r, w, ms)

    def dve(self, fn, r=(), w=()):
        self.op("dve", fn, r, w)

    def act(self, fn, r=(), w=()):
        self.op("act", fn, r, w)

    def dma(self, q, out, in_, r=(), w=(), **kw):
        r = [_b(x) for x in r]
        w = [_b(x) for x in w]
        slots = self.slots[q]
        i = self.slot_i[q]
        self.slot_i[q] = (i + 1) % len(slots)
        sl = slots[i]
        if sl[1] > 0:
            self._wait(q, (sl[2], sl[0], sl[1]))
        self._deps(q, r, w)
        sl[1] += 16
        self.eng[q].dma_start(out=out, in_=in_, **kw).then_inc(sl[0], 16)
        self.nins += 1
        tok = (sl[2], sl[0], sl[1])
        self._mark(tok, r, w)

    def barrier(self):
        toks = [(k, self.sem[k], self.cnt[k]) for k in self.cnt if self.cnt[k] > 0]
        for q in self.slots:
            for sl in self.slots[q]:
                if sl[1] > 0:
                    toks.append((sl[2], sl[0], sl[1]))
        for e in self.eng:
            for t in toks:
                self._wait(e, t)

    def tile(self, es, name, shape, dt):
        self.uid = getattr(self, "uid", 0) + 1
        return Tl(es.enter_context(self.nc.sbuf_tensor(f"{name}_{self.uid}", list(shape), dt)))

    def ps(self, nb=1):
        if self.pi % nb:
            self.pi += nb - self.pi % nb
        if self.pi + nb > 8:
            self.pi = 0
        i = self.pi
        self.pi += nb
        return self.psum[:, i * 512:(i + nb) * 512], self.pbuf[i:i + nb]


class View:
    def __init__(self, t, ap):
        self.buf = t.buf
        self.ap = ap

    def __getitem__(self, k):
        return self.ap


class ZRows:
    def __init__(self, a, b):
        self.a = a
        self.b = b

    def __getitem__(self, k):
        rs, cs = k
        if rs.start >= C_MG:
            return self.b[rs.start - C_MG:rs.stop - C_MG, cs]
        assert rs.stop <= C_MG
        return self.a[rs.start:rs.stop, cs]


class Ring:
    def __init__(self, tiles):
        self.t = tiles
        self.i = 0

    def next(self):
        t = self.t[self.i]
        self.i = (self.i + 1) % len(self.t)
        return t


class D:
    pass


def seqs_in_span(s0, n):
    if s0 < LAT:
        return [(s0, n, ("lat", s0 == 0, s0 + n == LAT, -1))]
    out = []
    for i in range(NCTX):
        out.append((LAT + i * CTXL, CTXL, ("ctx", True, True, i)))
    return out


SPANS = [(0, 1024), (1024, 1024), (2048, 1024), (3072, 1024), (4096, 1024)]


def span_order(d):
    return SPANS if d == 0 else [SPANS[3], SPANS[2], SPANS[1], SPANS[0], SPANS[4]]


def chunk_info(s0, c, d):
    t0 = s0 + c * 128
    if t0 < LAT:
        first = (t0 == 0) if d == 0 else (t0 + 128 == LAT)
        last = (t0 + 128 == LAT) if d == 0 else (t0 == 0)
        return first, last, "lat", -1
    r = (t0 - LAT) % CTXL
    i = (t0 - LAT) // CTXL
    first = (r == 0) if d == 0 else (r + 128 == CTXL)
    last = (r + 128 == CTXL) if d == 0 else (r == 0)
    return first, last, "ctx", i


WCONV = [("w_mod", 2048), ("w_in", 512), ("lru_gate_w", 2048), ("w_branch", 2048), ("w_out", 2048), ("ffn_w_up", 2048), ("ffn_w_down", 2048)]


def flat2(ap, F):
    nd = len(ap.shape)
    names = " ".join(f"d{i}" for i in range(nd))
    fl = ap.rearrange(f"{names} -> ({names})")
    return fl.rearrange("(b p f) -> b p f", p=128, f=F)


def phase_convert(P, g):
    with ExitStack() as es:
        fr = Ring([P.tile(es, f"cf{i}", [128, 2048], F32) for i in range(4)])
        br = Ring([P.tile(es, f"cb{i}", [128, 2048], BF16) for i in range(4)])
        k = 0
        for (name, F) in WCONV:
            src = flat2(getattr(g, name), F)
            dst = flat2(g.wbf[name], F)
            for b in range(src.shape[0]):
                f_, b_ = fr.next(), br.next()
                P.dma("sp", f_[:, :F], src[b], w=[f_])
                if k % 3 == 2:
                    P.dve(lambda e: e.tensor_copy(b_[:, :F], f_[:, :F]), r=[f_], w=[b_])
                else:
                    P.act(lambda e: e.copy(b_[:, :F], f_[:, :F]), r=[f_], w=[b_])
                P.dma("act", dst[b], b_[:, :F], r=[b_])
                k += 1

def phase_load_x(P, g):
    with ExitStack() as es:
        xin = Ring([P.tile(es, f"xin{i}", [128, DM], F32) for i in range(2)])
        xo = Ring([P.tile(es, f"xo{i}", [128, 16, 512], F32) for i in range(2)])
        k = 0
        for blk in range(T // 512):
            o = xo.next()
            for tt in range(4):
                t0 = blk * 512 + tt * 128
                xi = xin.next()
                P.dma("sp", xi[:], g.xin[t0:t0 + 128, :], w=[xi])
                for q in range(4):
                    ps, pb = P.ps(1)
                    for j in range(4):
                        c = q * 4 + j
                        P.pe(lambda e: e.transpose(ps[:, j * 128:(j + 1) * 128], xi[:, c * 128:(c + 1) * 128], g.identf[:]),
                             r=[xi, g.identf], w=pb, ms=(j == 3))
                    src = ps.rearrange("p (j t) -> p j t", j=4)
                    dst = o[:, q * 4:q * 4 + 4, tt * 128:(tt + 1) * 128]
                    if k % 2 == 0:
                        P.act(lambda e: e.copy(dst, src), r=pb, w=[o])
                    else:
                        P.dve(lambda e: e.tensor_copy(dst, src), r=pb, w=[o])
                    k += 1
            P.dma("act", g.XTv[:, :, blk * 512:(blk + 1) * 512], o[:], r=[o])


def phase_mod(P, g, l):
    nc = P.nc
    with ExitStack() as es:
        ct = P.tile(es, "ct", [128, 16, 2], F32)
        cs = P.tile(es, "cs", [128, 16, 2], BF16)
        bm = P.tile(es, "bm", [128, 96], F32)
        wm = Ring([P.tile(es, f"wm{i}", [128, 16, 512], BF16) for i in range(3)])
        for j in range(2):
            P.dma("sp", ct[:, :, j], g.cvec[j].rearrange("(k p) -> p k", p=128), w=[ct], allow_slow_non_contiguous=True)
        P.dma("sp", bm[:], g.b_mod[l].rearrange("(c p) -> p c", p=128), w=[bm], allow_slow_non_contiguous=True)
        P.dma("sp", g.n1g[:], g.norm1_g[l].rearrange("(c p) -> p c", p=128), w=[g.n1g], allow_slow_non_contiguous=True)
        P.dma("sp", g.n2g[:], g.norm2_g[l].rearrange("(c p) -> p c", p=128), w=[g.n2g], allow_slow_non_contiguous=True)
        P.act(lambda e: e.activation(cs[:], ct[:], AF.Silu), r=[ct], w=[cs])
        wv = g.w_mod[l].rearrange("(k p) n -> p k n", p=128)
        for nb in range(24):
            w = wm.next()
            P.dma("sp", w[:], wv[:, :, nb * 512:(nb + 1) * 512], w=[w])
            for cc in range(4):
                ps, pb = P.ps(1)
                for kc in range(16):
                    P.pe(lambda e: e.matmul(ps[:, 0:2], w[:, kc, cc * 128:(cc + 1) * 128], cs[:, kc, :],
                                            start=(kc == 0), stop=(kc == 15)),
                         r=[w, cs], w=pb, ms=(kc == 15))
                col = nb * 4 + cc
                P.dve(lambda e: e.tensor_scalar(g.mod[:, col, :], ps[:, 0:2], bm[:, col:col + 1], None, ALU.add),
                      r=pb + [bm], w=[g.mod])
        P.dve(lambda e: e.tensor_scalar(g.sA1[:], g.mod[:, 16:32, :], 1.0, None, ALU.add), r=[g.mod], w=[g.sA1])
        P.dve(lambda e: e.tensor_scalar(g.sA2[:], g.mod[:, 64:80, :], 1.0, None, ALU.add), r=[g.mod], w=[g.sA2])
        for j in range(2):
            P.dve(lambda e: e.tensor_tensor(g.sA1[:, :, j], g.sA1[:, :, j], g.n1g[:], ALU.mult), r=[g.sA1, g.n1g], w=[g.sA1])
            P.dve(lambda e: e.tensor_tensor(g.sA2[:, :, j], g.sA2[:, :, j], g.n2g[:], ALU.mult), r=[g.sA2, g.n2g], w=[g.sA2])


def norm_block(P, g, xb, nt, scale_fn, shift_fn, out_fn, sq):
    P.act(lambda e: e.activation(sq[:, :, :nt], xb[:, :, :nt], AF.Square), r=[xb], w=[sq])
    ps, pb = P.ps(1)
    for c in range(16):
        P.pe(lambda e: e.matmul(ps[:, :nt], g.onesf[:], sq[:, c, :nt], start=(c == 0), stop=(c == 15)),
             r=[g.onesf, sq], w=pb, ms=(c == 15))
    rs = g.rs_ring.next()
    P.act(lambda e: e.activation(rs[:, :nt], ps[:, :nt], AF.Sqrt, bias=g.epsc[:, 0:1], scale=1.0 / DM), r=pb + [g.epsc], w=[rs])
    P.dve(lambda e: e.reciprocal(rs[:, :nt], rs[:, :nt]), r=[rs], w=[rs])
    for c in range(16):
        P.dve(lambda e: e.tensor_tensor(xb[:, c, :nt], xb[:, c, :nt], rs[:, :nt], ALU.mult), r=[xb, rs], w=[xb])
        oap, ot = out_fn(c)
        sh = shift_fn(c)
        if sh is None:
            P.act(lambda e: e.activation(oap, xb[:, c, :nt], AF.Copy, scale=scale_fn(c)), r=[xb, g.mod, g.sA1, g.sA2, g.nfg], w=[ot])
        else:
            P.act(lambda e: e.activation(oap, xb[:, c, :nt], AF.Identity, bias=sh, scale=scale_fn(c)),
                  r=[xb, g.mod, g.sA1, g.sA2], w=[ot])


def build_uT(P, g, es_blk, uT, segs, which):
    sA = g.sA1 if which == 1 else g.sA2
    shb = 0 if which == 1 else 48
    loc = 0
    for (t0, n) in segs:
        o = 0
        while o < n:
            nt = min(NB, n - o)
            j = 0 if (t0 + o) < LAT else 1
            xb = g.xb_ring.next()
            P.dma("sp", xb[:, :, :nt], g.XTv[:, :, t0 + o:t0 + o + nt], w=[xb])
            lo = loc
            norm_block(P, g, xb, nt,
                       lambda c: sA[:, c, j:j + 1],
                       lambda c: g.mod[:, shb + c, j:j + 1],
                       lambda c: (uT[:, c, lo:lo + nt], uT),
                       g.sq)
            o += nt
            loc += nt


FSEG = ([(0, 512), (512, 1024), (2048, 2560), (2560, 3072), (3072, 3104), (3104, 3616), (3616, 4128),
         (4128, 4640), (4640, 5152), (6176, 6688), (6688, 7200), (7200, 7216), (7216, 7728), (7728, 8240),
         (8240, 8752), (8752, 9264)] + [(9264 + 512 * i, 9264 + 512 * (i + 1)) for i in range(12)])
VSEG = [(1024, 1536, 0), (1536, 2048, 512), (5152, 5664, 1024), (5664, 6176, 1536)]


def phase_stage_a(P, g, l):
    with ExitStack() as es:
        uT = P.tile(es, "uT", [128, 16, 2560], BF16)
        g.xb_ring = Ring([P.tile(es, f"xb{i}", [128, 16, NB], F32) for i in range(2)])
        g.sq = P.tile(es, "sq", [128, 16, NB], F32)
        wr = Ring([P.tile(es, f"wa{i}", [128, 16, 512], BF16) for i in range(2)])
        zs = Ring([P.tile(es, f"zs{i}", [128, 2560], F32) for i in range(2)])
        vs = Ring([P.tile(es, f"vs{i}", [128, 512], BF16) for i in range(3)])
        wv = g.w_in[l].rearrange("(k p) n -> p k n", p=128)
        k = 0
        for grp in range(2):
            g0 = grp * 2560
            build_uT(P, g, es, uT, [(g0, 2560)], 1)
            segs = [("F", a, b, 0) for (a, b) in FSEG] + [("V", a, b, vo) for (a, b, vo) in VSEG]
            for (kind, a, b, vo) in segs:
                wd = b - a
                w = wr.next()
                P.dma("sp", w[:, :, :wd], wv[:, :, a:b], w=[w])
                if kind == "F":
                    for cc in range((wd + 127) // 128):
                        wc = min(128, wd - cc * 128)
                        z = zs.next()
                        for tb in range(5):
                            ps, pb = P.ps(1)
                            for kc in range(16):
                                P.pe(lambda e: e.matmul(ps[:wc, :], w[:, kc, cc * 128:cc * 128 + wc], uT[:, kc, tb * 512:(tb + 1) * 512],
                                                        start=(kc == 0), stop=(kc == 15)),
                                     r=[w, uT], w=pb, ms=(kc == 15))
                            if k % 2 == 0:
                                P.act(lambda e: e.copy(z[:wc, tb * 512:(tb + 1) * 512], ps[:wc, :]), r=pb, w=[z])
                            else:
                                P.dve(lambda e: e.tensor_copy(z[:wc, tb * 512:(tb + 1) * 512], ps[:wc, :]), r=pb, w=[z])
                            k += 1
                        P.dma("act", g.ZT[a + cc * 128:a + cc * 128 + wc, g0:g0 + 2560], z[:wc, :], r=[z])
                else:
                    for tt in range(20):
                        ps, pb = P.ps(1)
                        for kc in range(16):
                            P.pe(lambda e: e.matmul(ps[:, :], uT[:, kc, tt * 128:(tt + 1) * 128], w[:, kc, :],
                                                    start=(kc == 0), stop=(kc == 15)),
                                 r=[w, uT], w=pb, ms=(kc == 15))
                        v = vs.next()
                        if k % 2 == 0:
                            P.act(lambda e: e.copy(v[:], ps[:, :]), r=pb, w=[v])
                        else:
                            P.dve(lambda e: e.tensor_copy(v[:], ps[:, :]), r=pb, w=[v])
                        k += 1
                        P.dma("act", g.VT[g0 + tt * 128:g0 + (tt + 1) * 128, vo:vo + 512], v[:], r=[v])


def mixer_epilogue(P, g, o, h, s0, gate_base, gate_func, ng, ybase, tl):
    gt, sqt, rs2, ybf = tl
    P.dma("sp", gt[:], g.ZT[gate_base + h * 256:gate_base + (h + 1) * 256, s0:s0 + 1024].rearrange("(v p) t -> p v t", p=128), w=[gt])
    P.act(lambda e: e.activation(sqt[:], o[:], AF.Square), r=[o], w=[sqt])
    ps, pb = P.ps(2)
    for half in range(2):
        for vc in range(2):
            P.pe(lambda e: e.matmul(ps[:, half * 512:(half + 1) * 512], g.onesf[:], sqt[:, vc, half * 512:(half + 1) * 512],
                                    start=(vc == 0), stop=(vc == 1)), r=[g.onesf, sqt], w=[pb[half]], ms=(vc == 1))
    P.act(lambda e: e.activation(rs2[:], ps, AF.Ln, bias=g.epsc[:, 0:1], scale=1.0 / 256), r=pb + [g.epsc], w=[rs2])
    P.act(lambda e: e.activation(rs2[:], rs2[:], AF.Exp, scale=-0.5), r=[rs2], w=[rs2])
    P.act(lambda e: e.activation(gt[:], gt[:], gate_func), r=[gt], w=[gt])
    for vc in range(2):
        P.dve(lambda e: e.tensor_tensor(o[:, vc, :], o[:, vc, :], rs2[:], ALU.mult), r=[o, rs2], w=[o])
        P.dve(lambda e: e.scalar_tensor_tensor(ybf[:, vc, :], o[:, vc, :], ng[:, h * 2 + vc:h * 2 + vc + 1], gt[:, vc, :],
                                               ALU.mult, ALU.mult), r=[o, ng, gt], w=[ybf])
    c0 = (ybase + h * 256) // 128
    P.dma("act", g.YTv[:, c0:c0 + 2, s0:s0 + 1024], ybf[:], r=[ybf])


def make_rmasks(P, es):
    rmF = P.tile(es, "rmF", [128, 1024], F32)
    rmB = P.tile(es, "rmB", [128, 1024], F32)
    P.dve(lambda e: e.memset(rmF[:], 1.0), w=[rmF])
    P.dve(lambda e: e.memset(rmF[:, 0::128], 0.0), r=[rmF], w=[rmF])
    P.dve(lambda e: e.memset(rmB[:], 1.0), w=[rmB])
    P.dve(lambda e: e.memset(rmB[:, 127::128], 0.0), r=[rmB], w=[rmB])
    return rmF, rmB


def phase_gla(P, g, l):
    with ExitStack() as es:
        t = lambda n, sh, dt: P.tile(es, n, sh, dt)
        wa = [t("gwa", [16, 512], F32) for _ in range(2)]
        nb = [t("gnb", [128, 4], F32) for _ in range(2)]
        ng = t("gng", [128, 8], F32)
        rm = make_rmasks(P, es)
        mask = [View(g.cst, g.cst[:, 256:384]), View(g.cst, g.cst[:, 384:512])]
        identb = t("identb", [128, 128], BF16)
        P.dve(lambda e: e.tensor_copy(identb[:], g.identf[:]), r=[g.identf], w=[identb])
        S = t("S", [128, 256], F32)
        Sbf = t("Sbf", [128, 256], BF16)
        ra = Ring([t("ra", [16, 1024], F32) for _ in range(2)])
        qr = Ring([t("q", [128, 1024], F32) for _ in range(2)])
        kr = Ring([t("k", [128, 1024], F32) for _ in range(2)])
        vr = Ring([t("v", [128, 8, 256], BF16) for _ in range(2)])
        sp = t("sp", [128, 1024], F32)
        Pc = t("Pc", [128, 1024], F32)
        tmp1 = t("tmp1", [128, 1024], F32)
        tmp2 = t("tmp2", [128, 1024], F32)
        qin = t("qin", [128, 1024], BF16)
        kin = t("kin", [128, 1024], BF16)
        kend = t("kend", [128, 1024], BF16)
        nbl = t("nbl", [128, 8], F32)
        dS = t("dS", [128, 8], F32)
        amr = Ring([t("am_all", [128, 8, 128], BF16) for _ in range(2)])
        ktr = Ring([t("kt_all", [128, 8, 128], BF16) for _ in range(2)])
        Ur = Ring([t("U_all", [128, 8, 256], F32) for _ in range(2)])
        Sbr = Ring([t("Sb_all", [128, 8, 256], BF16) for _ in range(2)])
        Sring = Ring([t("Sst", [128, 256], F32) for _ in range(6)])
        cur = [None]
        ofr = Ring([t("ofs", [128, 2, 1024], F32) for _ in range(2)])
        of0 = t("of0", [128, 2, 1024], F32)
        eptl = (t("gt", [128, 2, 1024], F32), t("sqt", [128, 2, 1024], F32), t("rs2", [128, 1024], F32),
                t("ybf", [128, 2, 1024], BF16))
        P.dma("sp", ng[:], g.gla_norm_g[l].rearrange("(c p) -> p c", p=128), w=[ng], allow_slow_non_contiguous=True)
        for d in range(2):
            P.dma("sp", wa[d][:], g.gla_w_alpha[l, d], w=[wa[d]])
            P.dma("sp", nb[d][:], g.gla_b_alpha[l, d].rearrange("(h p) -> p h", p=128), w=[nb[d]], allow_slow_non_contiguous=True)
            P.dve(lambda e: e.tensor_scalar(nb[d][:], nb[d][:], -1.0, None, ALU.mult), r=[nb[d]], w=[nb[d]])
        for d in range(2):
            lc = 127 if d == 0 else 0
            for h in range(4):
                for (s0, n) in span_order(d):
                    ra_t, qt, kt, vt = ra.next(), qr.next(), kr.next(), vr.next()
                    P.dma("sp", ra_t[:], g.ZT[C_RA + 16 * d:C_RA + 16 * d + 16, s0:s0 + 1024], w=[ra_t])
                    P.dma("sp", qt[:], g.ZT[C_QA + h * 128:C_QA + (h + 1) * 128, s0:s0 + 1024], w=[qt])
                    P.dma("sp", kt[:], g.ZT[C_KA + h * 128:C_KA + (h + 1) * 128, s0:s0 + 1024], w=[kt])
                    P.dma("sp", vt[:], g.VT[s0:s0 + 1024, h * 256:(h + 1) * 256].rearrange("(c p) v -> p c v", p=128), w=[vt])
                    if d == 1:
                        P.dma("sp", of0[:], g.OFv[:, h * 2:h * 2 + 2, s0:s0 + 1024], w=[of0])
                    ps, pb = P.ps(2)
                    for half in range(2):
                        P.pe(lambda e: e.matmul(ps[:, half * 512:(half + 1) * 512], wa[d][:, h * 128:(h + 1) * 128],
                                                ra_t[:, half * 512:(half + 1) * 512], start=True, stop=True),
                             r=[wa[d], ra_t], w=[pb[half]])
                    P.act(lambda e: e.activation(sp[:], ps, AF.Exp, bias=nb[d][:, h:h + 1], scale=-1.0), r=pb + [nb[d]], w=[sp])
                    P.act(lambda e: e.activation(sp[:], sp[:], AF.Ln, bias=1.0), r=[sp], w=[sp])
                    if d == 0:
                        P.dve(lambda e: e.tensor_tensor_scan(Pc[:], rm[0][:], sp[:], 0.0, ALU.mult, ALU.add), r=[rm[0], sp], w=[Pc])
                    else:
                        P.dve(lambda e: e.tensor_tensor_scan(Pc[:, ::-1], rm[1][:, ::-1], sp[:, ::-1], 0.0, ALU.mult, ALU.add),
                              r=[rm[1], sp], w=[Pc])
                    P.act(lambda e: e.activation(tmp1[:], Pc[:], AF.Exp, bias=g.lnqa[:, 0:1], scale=-1.0 / 16), r=[Pc, g.lnqa], w=[tmp1])
                    P.dve(lambda e: e.tensor_tensor(qin[:], qt[:], tmp1[:], ALU.mult), r=[qt, tmp1], w=[qin])
                    P.act(lambda e: e.activation(tmp2[:], Pc[:], AF.Exp, scale=1.0 / 16), r=[Pc], w=[tmp2])
                    P.dve(lambda e: e.tensor_tensor(kin[:], kt[:], tmp2[:], ALU.mult), r=[kt, tmp2], w=[kin])
                    P.dve(lambda e: e.tensor_scalar(nbl[:], Pc[:, lc::128], -1.0 / 16, None, ALU.mult), r=[Pc], w=[nbl])
                    P.act(lambda e: e.activation(dS[:], nbl[:], AF.Exp), r=[nbl], w=[dS])
                    for c in range(8):
                        P.act(lambda e: e.activation(tmp1[:, c * 128:(c + 1) * 128], Pc[:, c * 128:(c + 1) * 128], AF.Exp,
                                                     bias=nbl[:, c:c + 1], scale=1.0 / 16), r=[Pc, nbl], w=[tmp1])
                    P.dve(lambda e: e.tensor_tensor(kend[:], kt[:], tmp1[:], ALU.mult), r=[kt, tmp1], w=[kend])
                    ofs = ofr.next()
                    order = list(range(8)) if d == 0 else list(range(7, -1, -1))
                    am_all, kt_all, U_all, Sb_all = amr.next(), ktr.next(), Ur.next(), Sbr.next()
                    for c in order:
                        cs = slice(c * 128, (c + 1) * 128)
                        ps_a, pba = P.ps(1)
                        P.pe(lambda e: e.matmul(ps_a[:, 0:128], kin[:, cs], qin[:, cs], start=True, stop=True), r=[kin, qin], w=pba)
                        P.dve(lambda e: e.tensor_tensor(am_all[:, c, :], ps_a[:, 0:128], mask[d][:], ALU.mult), r=pba + [mask[d]], w=[am_all])
                    for c in order:
                        cs = slice(c * 128, (c + 1) * 128)
                        ps_t, pbt = P.ps(1)
                        pst = ps_t.bitcast(BF16)
                        P.pe(lambda e: e.transpose(pst[:, 0:128], kend[:, cs], identb[:]), r=[kend, identb], w=pbt)
                        P.act(lambda e: e.copy(kt_all[:, c, :], pst[:, 0:128]), r=pbt, w=[kt_all])
                    for i, c in enumerate(order):
                        ps_s, pbs = P.ps(1)
                        P.pe(lambda e: e.matmul(ps_s[:, 0:256], kt_all[:, c, :], vt[:, c, :], start=True, stop=True), r=[kt_all, vt], w=pbs)
                        if i % 2 == 0:
                            P.act(lambda e: e.copy(U_all[:, c, :], ps_s[:, 0:256]), r=pbs, w=[U_all])
                        else:
                            P.dve(lambda e: e.tensor_copy(U_all[:, c, :], ps_s[:, 0:256]), r=pbs, w=[U_all])
                    for c in order:
                        first, last, kind, ci = chunk_info(s0, c, d)
                        if first:
                            cur[0] = Sring.next()
                            S0 = cur[0]
                            if kind == "lat":
                                P.dma("sp", S0[:], g.st_gla[l, d, h], w=[S0])
                            else:
                                P.dve(lambda e: e.memset(S0[:], 0.0), w=[S0])
                        pre = cur[0]
                        P.act(lambda e: e.copy(Sb_all[:, c, :], pre[:]), r=[pre], w=[Sb_all])
                        nxt = Sring.next()
                        P.dve(lambda e: e.scalar_tensor_tensor(nxt[:], pre[:], dS[:, c:c + 1], U_all[:, c, :], ALU.mult, ALU.add),
                              r=[pre, dS, U_all], w=[nxt])
                        cur[0] = nxt
                        if last and kind == "ctx":
                            P.dma("act", g.o_gla[ci, l, d, h], nxt[:], r=[nxt])
                    for c in order:
                        cs = slice(c * 128, (c + 1) * 128)
                        ps_o, pbo = P.ps(1)
                        for vc in range(2):
                            vs_ = slice(vc * 128, (vc + 1) * 128)
                            P.pe(lambda e: e.matmul(ps_o[:, vs_], vt[:, c, vs_], am_all[:, c, :], start=True, stop=False), r=[vt, am_all], w=pbo, ms=False)
                            P.pe(lambda e: e.matmul(ps_o[:, vs_], Sb_all[:, c, vs_], qin[:, cs], start=False, stop=True), r=[Sb_all, qin], w=pbo, ms=True)
                        P.act(lambda e: e.copy(ofs[:, :, cs], ps_o[:, 0:256].rearrange("p (v t) -> p v t", v=2)), r=pbo, w=[ofs])
                    if d == 0:
                        P.dma("act", g.OFv[:, h * 2:h * 2 + 2, s0:s0 + 1024], ofs[:], r=[ofs])
                    else:
                        P.dve(lambda e: e.tensor_tensor(ofs[:], ofs[:], of0[:], ALU.add), r=[ofs, of0], w=[ofs])
                        mixer_epilogue(P, g, ofs, h, s0, C_GA, AF.Silu, ng, 0, eptl)
            if d == 0:
                P.barrier()


def rsl(a, b):
    return slice(b - 1, (a - 1) if a > 0 else None, -1)


def phase_mlstm(P, g, l):
    with ExitStack() as es:
        t = lambda n, sh, dt: P.tile(es, n, sh, dt)
        sel = t("sel", [16, 2048], F32)
        P.dma("sp", sel[:], g.selc[:, :], w=[sel])
        bif = t("bif", [128, 16], F32)
        nbif = t("nbif", [128, 16], F32)
        P.dma("sp", bif[:], g.mlstm_b_if[l].partition_broadcast(128), w=[bif])
        P.dve(lambda e: e.tensor_scalar(nbif[:], bif[:], -1.0, None, ALU.mult), r=[bif], w=[nbif])
        ng = t("mng", [128, 8], F32)
        P.dma("sp", ng[:], g.mlstm_norm_g[l].rearrange("(c p) -> p c", p=128), w=[ng], allow_slow_non_contiguous=True)
        rm = make_rmasks(P, es)
        mask = [View(g.cst, g.cst[:, 256:384]), View(g.cst, g.cst[:, 384:512])]
        identb = t("identb", [128, 128], BF16)
        P.dve(lambda e: e.tensor_copy(identb[:], g.identf[:]), r=[g.identf], w=[identb])
        onesb = t("onesb", [128, 128], BF16)
        P.dve(lambda e: e.tensor_copy(onesb[:], g.onesf[:]), r=[g.onesf], w=[onesb])
        zcol = t("zcol", [128, 1], F32)
        P.dve(lambda e: e.memset(zcol[:], 0.0), w=[zcol])
        mcr = Ring([t("mc", [128, 1], F32) for _ in range(2)])
        gbr = Ring([t("gb", [16, 1024], F32) for _ in range(1)])
        qr = Ring([t("q", [128, 2, 1024], F32) for _ in range(1)])
        kr = Ring([t("k", [128, 2, 1024], F32) for _ in range(1)])
        vr = Ring([t("v", [128, 8, 256], BF16) for _ in range(2)])
        ig, spf, lf, m, Pf, pm, pi = [t(n_, [128, 1024], F32) for n_ in ("ig", "spf", "lf", "m", "Pf", "pm", "pi")]
        eA, eB, efl, wp, kws = [t(n_, [128, 1024], F32) for n_ in ("eA", "eB", "efl", "wp", "kws")]
        qs, ks, qp, kw = [t(n_, [128, 2, 1024], BF16) for n_ in ("qs", "ks", "qp", "kw")]
        nbk = t("nbk", [128, 8], F32)
        ksum = t("ksum", [128, 2, 8], F32)
        smr = Ring([t("sm_all", [128, 8, 128], BF16) for _ in range(1)])
        dmx = Ring([t("dmx", [128, 128], F32) for _ in range(3)])
        kwtr = Ring([t("kwt_all", [128, 8, 2, 128], BF16) for _ in range(1)])
        Cbr = Ring([t("Cb_all", [128, 8, 512], BF16) for _ in range(1)])
        nbr = Ring([t("nb_all", [128, 8, 2, 128], BF16) for _ in range(1)])
        Ckr = Ring([t("Ck", [128, 4, 512], F32) for _ in range(1)])
        Cring = Ring([t("Cst", [128, 512], F32) for _ in range(6)])
        Nring = Ring([t("Nst", [128, 2], F32) for _ in range(6)])
        curC, curN = [None], [None]
        hsr = Ring([t("hs", [128, 2, 1024], F32) for _ in range(1)])
        of0 = t("of0", [128, 2, 1024], F32)
        eptl = (t("gt", [128, 2, 1024], F32), t("sqt", [128, 2, 1024], F32), t("rs2", [128, 1024], F32),
                t("ybf", [128, 2, 1024], BF16))
        for d in range(2):
            lc = 127 if d == 0 else 0
            for h in range(4):
                ri, rf = d * 8 + h, d * 8 + 4 + h
                mcar = None
                for (s0, n) in span_order(d):
                    gb, qt, kt, vt = gbr.next(), qr.next(), kr.next(), vr.next()
                    P.dma("sp", gb[:], g.ZT[C_GB:C_GB + 16, s0:s0 + 1024], w=[gb])
                    P.dma("sp", qt[:], g.ZT[C_QB + h * 256:C_QB + (h + 1) * 256, s0:s0 + 1024].rearrange("(c p) t -> p c t", p=128), w=[qt])
                    P.dma("sp", kt[:], g.ZT[C_KB + h * 256:C_KB + (h + 1) * 256, s0:s0 + 1024].rearrange("(c p) t -> p c t", p=128), w=[kt])
                    P.dma("sp", vt[:], g.VT[s0:s0 + 1024, 1024 + h * 256:1024 + (h + 1) * 256].rearrange("(c p) v -> p c v", p=128), w=[vt])
                    if d == 1:
                        P.dma("sp", of0[:], g.OFv[:, h * 2:h * 2 + 2, s0:s0 + 1024], w=[of0])
                    lat = s0 < LAT
                    if lat and mcar is None:
                        mcar = mcr.next()
                        P.dma("sp", mcar[:], g.st_mm[l, d, h:h + 1].partition_broadcast(128), w=[mcar])
                    ps, pb = P.ps(2)
                    for half in range(2):
                        P.pe(lambda e: e.matmul(ps[:, half * 512:(half + 1) * 512], sel[:, ri * 128:(ri + 1) * 128],
                                                gb[:, half * 512:(half + 1) * 512], start=True, stop=True), r=[sel, gb], w=[pb[half]])
                    P.act(lambda e: e.activation(ig[:], ps, AF.Identity, bias=bif[:, ri:ri + 1]), r=pb + [bif], w=[ig])
                    ps, pb = P.ps(2)
                    for half in range(2):
                        P.pe(lambda e: e.matmul(ps[:, half * 512:(half + 1) * 512], sel[:, rf * 128:(rf + 1) * 128],
                                                gb[:, half * 512:(half + 1) * 512], start=True, stop=True), r=[sel, gb], w=[pb[half]])
                    P.act(lambda e: e.activation(spf[:], ps, AF.Exp, bias=nbif[:, rf:rf + 1], scale=-1.0), r=pb + [nbif], w=[spf])
                    P.act(lambda e: e.activation(spf[:], spf[:], AF.Ln, bias=1.0), r=[spf], w=[spf])
                    P.dve(lambda e: e.tensor_scalar(lf[:], spf[:], -1.0, None, ALU.mult), r=[spf], w=[lf])
                    pieces = [(0, 1024)] if lat else [(i * CTXL, (i + 1) * CTXL) for i in range(NCTX)]
                    for (a, b) in pieces:
                        sl = slice(a, b) if d == 0 else rsl(a, b)
                        init = mcar[:, 0:1] if lat else 0.0
                        P.dve(lambda e: e.tensor_tensor_scan(m[:, sl], lf[:, sl], ig[:, sl], init, ALU.add, ALU.max),
                              r=[lf, ig] + ([mcar] if lat else []), w=[m])
                    mprev0 = mcar if lat else zcol
                    if lat:
                        mnext = mcr.next()
                        lastc = 1023 if d == 0 else 0
                        P.act(lambda e: e.copy(mnext[:], m[:, lastc:lastc + 1]), r=[m], w=[mnext])
                    if d == 0:
                        P.dve(lambda e: e.tensor_tensor_scan(Pf[:], rm[0][:], spf[:], 0.0, ALU.mult, ALU.add), r=[rm[0], spf], w=[Pf])
                    else:
                        P.dve(lambda e: e.tensor_tensor_scan(Pf[:, ::-1], rm[1][:, ::-1], spf[:, ::-1], 0.0, ALU.mult, ALU.add),
                              r=[rm[1], spf], w=[Pf])
                    P.dve(lambda e: e.tensor_tensor(pm[:], Pf[:], m[:], ALU.add), r=[Pf, m], w=[pm])
                    P.dve(lambda e: e.tensor_tensor(pi[:], Pf[:], ig[:], ALU.add), r=[Pf, ig], w=[pi])
                    P.act(lambda e: e.activation(eA[:], pm[:], AF.Exp, scale=-1.0), r=[pm], w=[eA])
                    P.act(lambda e: e.activation(eB[:], pi[:], AF.Exp, bias=g.lnkb[:, 0:1]), r=[pi, g.lnkb], w=[eB])
                    P.act(lambda e: e.activation(efl[:], m[:], AF.Exp, scale=-1.0), r=[m], w=[efl])
                    P.dve(lambda e: e.tensor_scalar(nbk[:], pm[:, lc::128], -1.0, LN_KB, ALU.mult, ALU.add), r=[pm], w=[nbk])
                    for c in range(8):
                        cs = slice(c * 128, (c + 1) * 128)
                        first, last, kind, ci = chunk_info(s0, c, d)
                        if first or (d == 0 and c == 0) or (d == 1 and c == 7):
                            mp, mpt = mprev0[:, 0:1], mprev0
                        else:
                            pc = (c - 1) * 128 + 127 if d == 0 else (c + 1) * 128
                            mp, mpt = m[:, pc:pc + 1], m
                        P.act(lambda e: e.activation(wp[:, cs], pm[:, cs], AF.Exp, bias=mp, scale=-1.0), r=[pm, mpt], w=[wp])
                        P.act(lambda e: e.activation(kws[:, cs], pi[:, cs], AF.Exp, bias=nbk[:, c:c + 1]), r=[pi, nbk], w=[kws])
                    for dc in range(2):
                        P.dve(lambda e: e.tensor_tensor(qs[:, dc, :], qt[:, dc, :], eA[:], ALU.mult), r=[qt, eA], w=[qs])
                        P.dve(lambda e: e.tensor_tensor(ks[:, dc, :], kt[:, dc, :], eB[:], ALU.mult), r=[kt, eB], w=[ks])
                        P.dve(lambda e: e.tensor_tensor(qp[:, dc, :], qt[:, dc, :], wp[:], ALU.mult), r=[qt, wp], w=[qp])
                        P.dve(lambda e: e.tensor_tensor(kw[:, dc, :], kt[:, dc, :], kws[:], ALU.mult), r=[kt, kws], w=[kw])
                    P.dve(lambda e: e.tensor_reduce(ksum[:], kw[:].rearrange("p d (c t) -> p d c t", t=128), AX.X, ALU.add), r=[kw], w=[ksum])
                    hs = hsr.next()
                    order = list(range(8)) if d == 0 else list(range(7, -1, -1))
                    sm_all, kwt_all, Cb_all, nb_all = smr.next(), kwtr.next(), Cbr.next(), nbr.next()
                    for c in order:
                        cs = slice(c * 128, (c + 1) * 128)
                        ps_s, pbs = P.ps(1)
                        for dc in range(2):
                            P.pe(lambda e: e.matmul(ps_s[:, 0:128], ks[:, dc, cs], qs[:, dc, cs], start=(dc == 0), stop=(dc == 1)),
                                 r=[ks, qs], w=pbs, ms=(dc == 1))
                        P.dve(lambda e: e.tensor_tensor(sm_all[:, c, :], ps_s[:, 0:128], mask[d][:], ALU.mult), r=pbs + [mask[d]], w=[sm_all])
                    for c in order:
                        cs = slice(c * 128, (c + 1) * 128)
                        ps_t, pbt = P.ps(1)
                        pst = ps_t.bitcast(BF16)
                        for dc in range(2):
                            P.pe(lambda e: e.transpose(pst[:, dc * 128:(dc + 1) * 128], kw[:, dc, cs], identb[:]), r=[kw, identb], w=pbt, ms=(dc == 1))
                        P.act(lambda e: e.copy(kwt_all[:, c, :, :], pst[:, 0:256].rearrange("p (d t) -> p d t", d=2)), r=pbt, w=[kwt_all])
                    for hf in range(2):
                        sub = order[hf * 4:(hf + 1) * 4]
                        Ck = Ckr.next()
                        for j, c in enumerate(sub):
                            ps_c, pbc = P.ps(1)
                            for dc in range(2):
                                P.pe(lambda e: e.matmul(ps_c[:, dc * 256:(dc + 1) * 256], kwt_all[:, c, dc, :], vt[:, c, :], start=True, stop=True),
                                     r=[kwt_all, vt], w=pbc, ms=(dc == 1))
                            if j % 2 == 0:
                                P.act(lambda e: e.copy(Ck[:, j, :], ps_c[:, 0:512]), r=pbc, w=[Ck])
                            else:
                                P.dve(lambda e: e.tensor_copy(Ck[:, j, :], ps_c[:, 0:512]), r=pbc, w=[Ck])
                        for j, c in enumerate(sub):
                            first, last, kind, ci = chunk_info(s0, c, d)
                            lcol = c * 128 + lc
                            if first:
                                curC[0], curN[0] = Cring.next(), Nring.next()
                                C0, N0 = curC[0], curN[0]
                                if kind == "lat":
                                    P.dma("sp", C0[:].rearrange("p (c v) -> p c v", c=2), g.st_mc[l, d, h].rearrange("(c p) v -> p c v", p=128), w=[C0])
                                    P.dma("sp", N0[:], g.st_mn[l, d, h].rearrange("(c p) -> p c", p=128), w=[N0], allow_slow_non_contiguous=True)
                                else:
                                    P.dve(lambda e: e.memset(C0[:], 0.0), w=[C0])
                                    P.dve(lambda e: e.memset(N0[:], 0.0), w=[N0])
                            preC, preN = curC[0], curN[0]
                            P.act(lambda e: e.copy(Cb_all[:, c, :], preC[:]), r=[preC], w=[Cb_all])
                            for dc in range(2):
                                P.act(lambda e: e.activation(nb_all[:, c, dc, :], g.onesf[:], AF.Copy, scale=preN[:, dc:dc + 1]), r=[g.onesf, preN], w=[nb_all])
                            nC, nN = Cring.next(), Nring.next()
                            dec = wp[:, lcol:lcol + 1]
                            P.dve(lambda e: e.scalar_tensor_tensor(nC[:], preC[:], dec, Ck[:, j, :], ALU.mult, ALU.add), r=[preC, wp, Ck], w=[nC])
                            P.dve(lambda e: e.scalar_tensor_tensor(nN[:], preN[:], dec, ksum[:, :, c], ALU.mult, ALU.add), r=[preN, wp, ksum], w=[nN])
                            curC[0], curN[0] = nC, nN
                            if last and kind == "ctx":
                                P.dma("act", g.o_mc[ci, l, d, h].rearrange("(c p) v -> p c v", p=128), nC[:].rearrange("p (c v) -> p c v", c=2), r=[nC])
                                P.dma("act", g.o_mn[ci, l, d, h].rearrange("(c p) -> p c", p=128), nN[:], r=[nN], allow_slow_non_contiguous=True)
                                P.dma("act", g.o_mm[ci, l, d, h:h + 1].rearrange("(a b) -> a b", a=1), m[0:1, lcol:lcol + 1], r=[m])
                    for c in order:
                        cs = slice(c * 128, (c + 1) * 128)
                        ps_d, pbd = P.ps(1)
                        P.pe(lambda e: e.matmul(ps_d[:, 0:128], onesb[:], sm_all[:, c, :], start=True, stop=False), r=[onesb, sm_all], w=pbd, ms=False)
                        for dc in range(2):
                            P.pe(lambda e: e.matmul(ps_d[:, 0:128], nb_all[:, c, dc, :], qp[:, dc, cs], start=False, stop=(dc == 1)),
                                 r=[nb_all, qp], w=pbd, ms=(dc == 1))
                        ps_n, pbn = P.ps(1)
                        for vc in range(2):
                            vs_ = slice(vc * 128, (vc + 1) * 128)
                            P.pe(lambda e: e.matmul(ps_n[:, vs_], vt[:, c, vs_], sm_all[:, c, :], start=True, stop=False), r=[vt, sm_all], w=pbn, ms=False)
                            for dc in range(2):
                                P.pe(lambda e: e.matmul(ps_n[:, vs_], Cb_all[:, c, dc * 256 + vc * 128:dc * 256 + (vc + 1) * 128], qp[:, dc, cs],
                                                        start=False, stop=(dc == 1)), r=[Cb_all, qp], w=pbn, ms=(dc == 1))
                        dm = dmx.next()
                        P.act(lambda e: e.activation(dm[:], ps_d[:, 0:128], AF.Abs), r=pbd, w=[dm])
                        P.dve(lambda e: e.tensor_tensor(dm[:], dm[:], efl[:, cs], ALU.max), r=[dm, efl], w=[dm])
                        P.dve(lambda e: e.reciprocal(dm[:], dm[:]), r=[dm], w=[dm])
                        for vc in range(2):
                            P.dve(lambda e: e.tensor_tensor(hs[:, vc, cs], ps_n[:, vc * 128:(vc + 1) * 128], dm[:], ALU.mult), r=pbn + [dm], w=[hs])
                    if lat:
                        mcar = mnext
                    if d == 0:
                        P.dma("act", g.OFv[:, h * 2:h * 2 + 2, s0:s0 + 1024], hs[:], r=[hs])
                    else:
                        P.dve(lambda e: e.tensor_tensor(hs[:], hs[:], of0[:], ALU.add), r=[hs, of0], w=[hs])
                        mixer_epilogue(P, g, hs, h, s0, C_OB, AF.Sigmoid, ng, 1024, eptl)
            if d == 0:
                P.barrier()


def phase_lru(P, g, l):
    with ExitStack() as es:
        t = lambda n, sh, dt: P.tile(es, n, sh, dt)
        cw = t("cw", [128, 8, 4], F32)
        cb = t("cb", [128, 8], F32)
        gbias = t("gbias", [128, 4, 8], F32)
        lam = t("lam", [128, 2, 8], F32)
        c1 = t("c1", [128, 2, 8], F32)
        c2 = t("c2", [128, 2, 8], F32)
        gw = t("gw", [128, 32, 128], BF16)
        for j in range(4):
            P.dma("sp", cw[:, :, j], g.lru_conv_w[l, j].rearrange("(c p) -> p c", p=128), w=[cw], allow_slow_non_contiguous=True)
        P.dma("sp", cb[:], g.lru_conv_b[l].rearrange("(c p) -> p c", p=128), w=[cb], allow_slow_non_contiguous=True)
        for d in range(2):
            for gi in range(2):
                P.dma("sp", gbias[:, d * 2 + gi, :], g.lru_gate_b[l, d, gi].rearrange("(c p) -> p c", p=128), w=[gbias], allow_slow_non_contiguous=True)
            P.dma("sp", lam[:, d, :], g.lru_lambda[l, d].rearrange("(c p) -> p c", p=128), w=[lam], allow_slow_non_contiguous=True)
        gwv = g.lru_gate_w[l].rearrange("d g n k j -> k (d g n) j")
        for q in range(4):
            P.dma("sp", gw[:, q * 8:(q + 1) * 8, :], gwv[:, q * 8:(q + 1) * 8, :], w=[gw])
        P.act(lambda e: e.activation(c1[:], lam[:], AF.Exp, scale=-1.0), r=[lam], w=[c1])
        P.act(lambda e: e.activation(c1[:], c1[:], AF.Ln, bias=1.0), r=[c1], w=[c1])
        P.dve(lambda e: e.tensor_scalar(c2[:], c1[:], -16.0, None, ALU.mult), r=[c1], w=[c2])
        P.dve(lambda e: e.tensor_scalar(c1[:], c1[:], -8.0, None, ALU.mult), r=[c1], w=[c1])
        x = t("x", [128, T], F32)
        xc = t("xc", [128, T], F32)
        xcb = t("xcb", [128, T], BF16)
        HS = t("HS", [128, T], F32)
        yr = t("yr", [128, T], F32)
        ybf = t("ybf", [128, T], BF16)
        HS1 = t("HS1", [128, T], F32)
        tmps = [[t(n_, [128, 1024], F32) for n_ in ("r_", "i_", "a_", "e2", "tt", "hd")] for _ in range(2)]
        hcrs = [Ring([t("hc", [128, 1], F32) for _ in range(2)]) for _ in range(2)]
        seqs = [(0, LAT)] + [(LAT + i * CTXL, CTXL) for i in range(NCTX)]
        for cc in range(8):
            P.dma("sp", x[:], g.ZT[C_XR + cc * 128:C_XR + (cc + 1) * 128, 0:T], w=[x])
            P.dma("sp", yr[:], g.ZT[C_YR + cc * 128:C_YR + (cc + 1) * 128, 0:T], w=[yr])
            P.act(lambda e: e.activation(xc[:], x[:], AF.Identity, bias=cb[:, cc:cc + 1], scale=cw[:, cc, 2:3]), r=[x, cb, cw], w=[xc])
            for (t0, n) in seqs:
                for j in (0, 1, 3):
                    sh = j - 2
                    lo, hi = t0 + max(0, -sh), t0 + n - max(0, sh)
                    P.dve(lambda e: e.scalar_tensor_tensor(xc[:, lo:hi], x[:, lo + sh:hi + sh], cw[:, cc, j:j + 1], xc[:, lo:hi],
                                                           ALU.mult, ALU.add), r=[x, cw, xc], w=[xc])
            P.act(lambda e: e.copy(xcb[:], xc[:]), r=[xc], w=[xcb])
            hcars = [None, None]
            for idx in range(5):
                for d in range(2):
                    (s0, n) = span_order(d)[idx]
                    r_, i_, a_, e2, tt, hd = tmps[d]
                    hcr = hcrs[d]
                    hcar = hcars[d]
                    lat = s0 < LAT
                    ss = slice(s0, s0 + 1024)
                    pr, pbr = P.ps(2)
                    pi_, pbi = P.ps(2)
                    for gi, (pp, pbb) in enumerate(((pr, pbr), (pi_, pbi))):
                        for half in range(2):
                            P.pe(lambda e: e.matmul(pp[:, half * 512:(half + 1) * 512], gw[:, (d * 2 + gi) * 8 + cc, :],
                                                    xcb[:, s0 + half * 512:s0 + (half + 1) * 512], start=True, stop=True),
                                 r=[gw, xcb], w=[pbb[half]])
                    P.act(lambda e: e.activation(r_[:], pr, AF.Sigmoid, bias=gbias[:, d * 2, cc:cc + 1]), r=pbr + [gbias], w=[r_])
                    P.act(lambda e: e.activation(i_[:], pi_, AF.Sigmoid, bias=gbias[:, d * 2 + 1, cc:cc + 1]), r=pbi + [gbias], w=[i_])
                    P.act(lambda e: e.activation(a_[:], r_[:], AF.Exp, scale=c1[:, d, cc:cc + 1]), r=[r_, c1], w=[a_])
                    P.act(lambda e: e.activation(e2[:], r_[:], AF.Exp, scale=c2[:, d, cc:cc + 1]), r=[r_, c2], w=[e2])
                    P.act(lambda e: e.activation(e2[:], e2[:], AF.Sqrt, bias=1.0, scale=-1.0), r=[e2], w=[e2])
                    P.dve(lambda e: e.tensor_tensor(tt[:], i_[:], xc[:, ss], ALU.mult), r=[i_, xc], w=[tt])
                    P.dve(lambda e: e.tensor_tensor(tt[:], tt[:], e2[:], ALU.mult), r=[tt, e2], w=[tt])
                    if lat and hcar is None:
                        hcar = hcr.next()
                        P.dma("sp", hcar[:], g.st_lru[l, d, cc * 128:(cc + 1) * 128].rearrange("(p a) -> p a", a=1), w=[hcar],
                              allow_slow_non_contiguous=True)
                    pieces = [(0, 1024)] if lat else [(i * CTXL, (i + 1) * CTXL) for i in range(NCTX)]
                    for (a, b) in pieces:
                        sl = slice(a, b) if d == 0 else rsl(a, b)
                        init = hcar[:, 0:1] if lat else 0.0
                        P.dve(lambda e: e.tensor_tensor_scan(hd[:, sl], a_[:, sl], tt[:, sl], init, ALU.mult, ALU.add),
                              r=[a_, tt] + ([hcar] if lat else []), w=[hd])
                    if lat:
                        hn = hcr.next()
                        lastc = 1023 if d == 0 else 0
                        P.act(lambda e: e.copy(hn[:], hd[:, lastc:lastc + 1]), r=[hd], w=[hn])
                        hcar = hn
                    else:
                        for i in range(NCTX):
                            col = (i + 1) * CTXL - 1 if d == 0 else i * CTXL
                            P.dma("act", g.o_lru[i, l, d, cc * 128:(cc + 1) * 128].rearrange("(p a) -> p a", a=1), hd[:, col:col + 1], r=[hd],
                                  allow_slow_non_contiguous=True)
                    hcars[d] = hcar
                    HSx = HS if d == 0 else HS1
                    P.act(lambda e: e.copy(HSx[:, ss], hd[:]), r=[hd], w=[HSx])
            P.dve(lambda e: e.tensor_tensor(HS[:], HS[:], HS1[:], ALU.add), r=[HS, HS1], w=[HS])
            P.act(lambda e: e.activation(x[:], yr[:], AF.Square), r=[yr], w=[x])
            P.dve(lambda e: e.tensor_scalar(x[:], x[:], 0.044715, 1.0, ALU.mult, ALU.add), r=[x], w=[x])
            P.dve(lambda e: e.tensor_tensor(x[:], x[:], yr[:], ALU.mult), r=[x, yr], w=[x])
            P.act(lambda e: e.activation(x[:], x[:], AF.Sigmoid, scale=1.5957691216057308), r=[x], w=[x])
            P.dve(lambda e: e.tensor_tensor(yr[:], yr[:], x[:], ALU.mult), r=[yr, x], w=[yr])
            P.dve(lambda e: e.tensor_tensor(ybf[:], HS[:], yr[:], ALU.mult), r=[HS, yr], w=[ybf])
            P.dma("act", g.YT[2048 + cc * 128:2048 + (cc + 1) * 128, 0:T], ybf[:], r=[ybf])


def phase_merge(P, g, l):
    BT = 256
    with ExitStack() as es:
        t = lambda n, sh, dt: P.tile(es, n, sh, dt)
        Wbr = t("Wbr", [128, 24, DM], BF16)
        bmg = t("bmg", [128, 48], F32)
        for n in range(3):
            for hf in range(2):
                P.dma("sp", Wbr[:, n * 8:(n + 1) * 8, hf * 1024:(hf + 1) * 1024],
                      g.w_branch[l, n].rearrange("(k p) c -> p k c", p=128)[:, :, hf * 1024:(hf + 1) * 1024], w=[Wbr])
        P.dma("sp", bmg[:], g.b_merge[l].rearrange("(c p) -> p c", p=128), w=[bmg], allow_slow_non_contiguous=True)
        Yr = Ring([t("Yb", [128, 24, BT], BF16) for _ in range(2)])
        Mr = Ring([t("Mb", [128, 16, BT], BF16) for _ in range(2)])
        mgr = Ring([t("mg", [128, BT], F32) for _ in range(6)])
        accr = Ring([t("acc", [128, BT], F32) for _ in range(2)])
        tr = Ring([t("tq", [128, BT], F32) for _ in range(2)])
        for blk in range(T // BT):
            t0 = blk * BT
            Y = Yr.next()
            P.dma("sp", Y[:], g.YTv[:, :, t0:t0 + BT], w=[Y])
            Mb = Mr.next()
            for colc in range(16):
                mgs = []
                for n in range(3):
                    mg = mgr.next()
                    r0 = C_MG + n * DM + colc * 128
                    P.dma("sp", mg[:], g.ZT[r0:r0 + 128, t0:t0 + BT], w=[mg])
                    P.act(lambda e: e.activation(mg[:], mg[:], AF.Sigmoid, bias=bmg[:, n * 16 + colc:n * 16 + colc + 1]), r=[mg, bmg], w=[mg])
                    mgs.append(mg)
                pss = []
                for n in range(3):
                    ps, pb = P.ps(1)
                    for kc in range(8):
                        P.pe(lambda e: e.matmul(ps[:, :BT], Wbr[:, n * 8 + kc, colc * 128:(colc + 1) * 128], Y[:, n * 8 + kc, :],
                                                start=(kc == 0), stop=(kc == 7)), r=[Wbr, Y], w=pb, ms=(kc == 7))
                    pss.append((ps, pb))
                acc, tq = accr.next(), tr.next()
                P.dve(lambda e: e.tensor_tensor(acc[:], pss[0][0][:, :BT], mgs[0][:], ALU.mult), r=pss[0][1] + [mgs[0]], w=[acc])
                P.dve(lambda e: e.tensor_tensor(tq[:], pss[1][0][:, :BT], mgs[1][:], ALU.mult), r=pss[1][1] + [mgs[1]], w=[tq])
                P.dve(lambda e: e.tensor_tensor(acc[:], acc[:], tq[:], ALU.add), r=[acc, tq], w=[acc])
                P.dve(lambda e: e.tensor_tensor(tq[:], pss[2][0][:, :BT], mgs[2][:], ALU.mult), r=pss[2][1] + [mgs[2]], w=[tq])
                P.dve(lambda e: e.tensor_tensor(Mb[:, colc, :], acc[:], tq[:], ALU.add), r=[acc, tq], w=[Mb])
            P.dma("act", g.MTv[:, :, t0:t0 + BT], Mb[:], r=[Mb])


def phase_out(P, g, l):
    with ExitStack() as es:
        t = lambda n, sh, dt: P.tile(es, n, sh, dt)
        Wo = t("Wo", [128, 16, DM], BF16)
        for hf in range(2):
            P.dma("sp", Wo[:, :, hf * 1024:(hf + 1) * 1024],
                  g.w_out[l].rearrange("(k p) c -> p k c", p=128)[:, :, hf * 1024:(hf + 1) * 1024], w=[Wo])
        Mr = Ring([t("Mb", [128, 16, 512], BF16) for _ in range(2)])
        Xr = Ring([t("xb", [128, 16, 512], F32) for _ in range(2)])
        for blk in range(T // 512):
            t0 = blk * 512
            j = 0 if t0 < LAT else 1
            Mb, xb = Mr.next(), Xr.next()
            P.dma("sp", Mb[:], g.MTv[:, :, t0:t0 + 512], w=[Mb])
            P.dma("sp", xb[:], g.XTv[:, :, t0:t0 + 512], w=[xb])
            for colc in range(16):
                ps, pb = P.ps(1)
                for kc in range(16):
                    P.pe(lambda e: e.matmul(ps[:, :], Wo[:, kc, colc * 128:(colc + 1) * 128], Mb[:, kc, :], start=(kc == 0), stop=(kc == 15)),
                         r=[Wo, Mb], w=pb, ms=(kc == 15))
                P.dve(lambda e: e.scalar_tensor_tensor(xb[:, colc, :], ps[:, :], g.mod[:, 32 + colc, j:j + 1], xb[:, colc, :], ALU.mult, ALU.add),
                      r=pb + [g.mod, xb], w=[xb])
            P.dma("act", g.XTv[:, :, t0:t0 + 512], xb[:], r=[xb])


FFN_GROUPS = [
    ([(0, 2112), (4096, 512)], (0, 32), 0, 4096),
    ([(1984, 2112), (4608, 512)], (1, 33), 2048, 4608),
]


def phase_ffn_up(P, g, l):
    NL = 2624
    with ExitStack() as es:
        t = lambda n, sh, dt: P.tile(es, n, sh, dt)
        vT = t("vT", [128, 16, NL], BF16)
        fcw = t("fcw", [128, 44, 9], F32)
        fcb = t("fcb", [128, 44], F32)
        for j in range(9):
            P.dma("sp", fcw[:, :, j], g.ffn_conv_w[l, j].rearrange("(c p) -> p c", p=128), w=[fcw], allow_slow_non_contiguous=True)
        P.dma("sp", fcb[:], g.ffn_conv_b[l].rearrange("(c p) -> p c", p=128), w=[fcb], allow_slow_non_contiguous=True)
        wv = g.ffn_w_up[l].rearrange("(k p) n -> p k n", p=128)
        for (segs, (r0, r1), lat_out, ctx_out) in FFN_GROUPS:
            with ExitStack() as es2:
                g.xb_ring = Ring([P.tile(es2, f"xb{i}", [128, 16, NB], F32) for i in range(2)])
                g.sq = P.tile(es2, "sq", [128, 16, NB], F32)
                build_uT(P, g, es2, vT, segs, 2)
                P.barrier()
            with ExitStack() as es2:
                t2 = lambda n, sh, dt: P.tile(es2, n, sh, dt)
                Wr = Ring([t2("Wu", [128, 16, 256], BF16) for _ in range(2)])
                hg, hu, acc = t2("hg", [128, NL], F32), t2("hu", [128, NL], F32), t2("acc", [128, NL], F32)
                pr = Ring([t2("pbf", [128, 2560], BF16) for _ in range(2)])
                tpr = Ring([t2("tp", [128, 2112], F32) for _ in range(2)])
                hg3 = hg[:, 0:2112].rearrange("p (r c) -> p r c", c=64)
                acc3 = acc[:, 0:2112].rearrange("p (r c) -> p r c", c=64)
                tbs = [(o, 512) for o in range(0, 2560, 512)] + [(2560, 64)]
                for cc in range(44):
                    W = Wr.next()
                    P.dma("sp", W[:, :, 0:128], wv[:, :, cc * 128:(cc + 1) * 128], w=[W])
                    P.dma("sp", W[:, :, 128:256], wv[:, :, D_FF + cc * 128:D_FF + (cc + 1) * 128], w=[W])
                    for (o, nt) in tbs:
                        psg, pbg = P.ps(1)
                        for kc in range(16):
                            P.pe(lambda e: e.matmul(psg[:, :nt], W[:, kc, 0:128], vT[:, kc, o:o + nt], start=(kc == 0), stop=(kc == 15)),
                                 r=[W, vT], w=pbg, ms=(kc == 15))
                        P.act(lambda e: e.copy(hg[:, o:o + nt], psg[:, :nt]), r=pbg, w=[hg])
                        psu, pbu = P.ps(1)
                        for kc in range(16):
                            P.pe(lambda e: e.matmul(psu[:, :nt], W[:, kc, 128:256], vT[:, kc, o:o + nt], start=(kc == 0), stop=(kc == 15)),
                                 r=[W, vT], w=pbu, ms=(kc == 15))
                        P.act(lambda e: e.copy(hu[:, o:o + nt], psu[:, :nt]), r=pbu, w=[hu])
                    la, lb = r0 * 64, r1 * 64
                    P.act(lambda e: e.activation(acc[:, la:lb], hg[:, la:lb], AF.Identity, bias=fcb[:, cc:cc + 1], scale=fcw[:, cc, 4:5]),
                          r=[hg, fcb, fcw], w=[acc])
                    for dy in (-1, 0, 1):
                        for dx in (-1, 0, 1):
                            if dy == 0 and dx == 0:
                                continue
                            ra, rb = max(r0, -dy), min(r1, 33 - dy)
                            ca, cb_ = max(0, -dx), min(64, 64 - dx)
                            wi = (dy + 1) * 3 + (dx + 1)
                            if wi % 2 == 1:
                                tp = tpr.next()
                                tp3 = tp[:, 0:2112].rearrange("p (r c) -> p r c", c=64)
                                P.act(lambda e: e.activation(tp3[:, ra:rb, ca:cb_], hg3[:, ra + dy:rb + dy, ca + dx:cb_ + dx], AF.Copy,
                                                             scale=fcw[:, cc, wi:wi + 1]), r=[hg, fcw], w=[tp])
                                P.dve(lambda e: e.tensor_tensor(acc3[:, ra:rb, ca:cb_], acc3[:, ra:rb, ca:cb_], tp3[:, ra:rb, ca:cb_], ALU.add),
                                      r=[acc, tp], w=[acc])
                                continue
                            P.dve(lambda e: e.scalar_tensor_tensor(acc3[:, ra:rb, ca:cb_], hg3[:, ra + dy:rb + dy, ca + dx:cb_ + dx],
                                                                   fcw[:, cc, wi:wi + 1], acc3[:, ra:rb, ca:cb_], ALU.mult, ALU.add),
                                  r=[hg, fcw, acc], w=[acc])
                    P.act(lambda e: e.activation(acc[:, 2112:NL], hg[:, 2112:NL], AF.Identity, bias=fcb[:, cc:cc + 1], scale=fcw[:, cc, 4:5]),
                          r=[hg, fcb, fcw], w=[acc])
                    for si in range(2):
                        q0 = 2112 + si * CTXL
                        for dx in (-1, 1):
                            lo, hi = q0 + max(0, -dx), q0 + CTXL - max(0, dx)
                            wi = 3 + (dx + 1)
                            P.dve(lambda e: e.scalar_tensor_tensor(acc[:, lo:hi], hg[:, lo + dx:hi + dx], fcw[:, cc, wi:wi + 1], acc[:, lo:hi],
                                                                   ALU.mult, ALU.add), r=[hg, fcw, acc], w=[acc])
                    pbf = pr.next()
                    P.act(lambda e: e.activation(acc[:, la:lb], acc[:, la:lb], AF.Silu), r=[acc], w=[acc])
                    P.act(lambda e: e.activation(acc[:, 2112:NL], acc[:, 2112:NL], AF.Silu), r=[acc], w=[acc])
                    P.dve(lambda e: e.tensor_tensor(pbf[:, 0:2048], acc[:, la:lb], hu[:, la:lb], ALU.mult), r=[acc, hu], w=[pbf])
                    P.dve(lambda e: e.tensor_tensor(pbf[:, 2048:2560], acc[:, 2112:NL], hu[:, 2112:NL], ALU.mult), r=[acc, hu], w=[pbf])
                    P.dma("act", g.PT[cc * 128:(cc + 1) * 128, lat_out:lat_out + 2048], pbf[:, 0:2048], r=[pbf])
                    P.dma("act", g.PT[cc * 128:(cc + 1) * 128, ctx_out:ctx_out + 512], pbf[:, 2048:2560], r=[pbf])
                P.barrier()


def phase_ffn_down(P, g, l):
    with ExitStack() as es:
        t = lambda n, sh, dt: P.tile(es, n, sh, dt)
        Pb = t("Pb", [128, 44, 1024], BF16)
        Wr = Ring([t("Wd", [128, 44, 256], BF16) for _ in range(2)])
        Xr = Ring([t("xd", [128, 1024], F32) for _ in range(3)])
        wv = g.ffn_w_down[l].rearrange("(k p) c -> p k c", p=128)
        PTv = g.PT.rearrange("(k p) t -> p k t", p=128)
        for blk in range(T // 1024):
            t0 = blk * 1024
            j = 0 if t0 < LAT else 1
            for q in range(4):
                P.dma("sp", Pb[:, q * 11:(q + 1) * 11, :], PTv[:, q * 11:(q + 1) * 11, t0:t0 + 1024], w=[Pb])
            for c2 in range(8):
                W = Wr.next()
                for q in range(4):
                    P.dma("sp", W[:, q * 11:(q + 1) * 11, :], wv[:, q * 11:(q + 1) * 11, c2 * 256:(c2 + 1) * 256], w=[W])
                for ci in range(2):
                    colc = c2 * 2 + ci
                    xb = Xr.next()
                    P.dma("sp", xb[:], g.XT[colc * 128:(colc + 1) * 128, t0:t0 + 1024], w=[xb])
                    for half in range(2):
                        ps, pb = P.ps(1)
                        for kc in range(44):
                            P.pe(lambda e: e.matmul(ps[:, :], W[:, kc, ci * 128:(ci + 1) * 128], Pb[:, kc, half * 512:(half + 1) * 512],
                                                    start=(kc == 0), stop=(kc == 43)), r=[W, Pb], w=pb, ms=(kc == 43))
                        hs_ = slice(half * 512, (half + 1) * 512)
                        P.dve(lambda e: e.scalar_tensor_tensor(xb[:, hs_], ps[:, :], g.mod[:, 80 + colc, j:j + 1], xb[:, hs_], ALU.mult, ALU.add),
                              r=pb + [g.mod, xb], w=[xb])
                    P.dma("act", g.XT[colc * 128:(colc + 1) * 128, t0:t0 + 1024], xb[:], r=[xb])


def phase_final(P, g):
    with ExitStack() as es:
        t = lambda n, sh, dt: P.tile(es, n, sh, dt)
        P.dma("sp", g.nfg[:], g.norm_f_g.rearrange("(c p) -> p c", p=128), w=[g.nfg], allow_slow_non_contiguous=True)
        g.xb_ring = Ring([t(f"xb{i}", [128, 16, NB], F32) for i in range(2)])
        g.sq = t("sq", [128, 16, NB], F32)
        ybr = Ring([t("yb", [128, 16, NB], F32) for _ in range(2)])
        yor = Ring([t("yo", [128, DM], F32) for _ in range(2)])
        k = 0
        for blk in range(T // NB):
            t0 = blk * NB
            xb = g.xb_ring.next()
            yb = ybr.next()
            P.dma("sp", xb[:], g.XTv[:, :, t0:t0 + NB], w=[xb])
            norm_block(P, g, xb, NB, lambda c: g.nfg[:, c:c + 1], lambda c: None, lambda c: (yb[:, c, :], yb), g.sq)
            for tt in range(NB // 128):
                yo = yor.next()
                for q in range(4):
                    ps, pb = P.ps(1)
                    for j in range(4):
                        c = q * 4 + j
                        P.pe(lambda e: e.transpose(ps[:, j * 128:(j + 1) * 128], yb[:, c, tt * 128:(tt + 1) * 128], g.identf[:]),
                             r=[yb, g.identf], w=pb, ms=(j == 3))
                    if k % 2 == 0:
                        P.act(lambda e: e.copy(yo[:, q * 512:(q + 1) * 512], ps[:, :]), r=pb, w=[yo])
                    else:
                        P.dve(lambda e: e.tensor_copy(yo[:, q * 512:(q + 1) * 512], ps[:, :]), r=pb, w=[yo])
                    k += 1
                P.dma("act", g.y[t0 + tt * 128:t0 + (tt + 1) * 128, :], yo[:], r=[yo])

def build_program():
    nc = bass.Bass("TRN2", target_bir_lowering=False)
    g = D()

    def din(name, shape):
        return nc.dram_tensor(name, list(shape), F32, kind="ExternalInput").ap()

    def dout(name, shape):
        return nc.dram_tensor(name, list(shape), F32, kind="ExternalOutput").ap()

    def scratch(name, shape, dt):
        kind = "ExternalOutput" if name in DUMP else "Internal"
        return nc.dram_tensor(name, list(shape), dt, kind=kind).ap()

    g.xin = din("xin", [T, DM])
    g.cvec = din("cvec", [2, DM])
    g.consts = din("consts", [128, 512])
    g.selc = din("selc", [16, 2048])
    g.st_gla = din("st_gla", [DEPTH, 2, 4, 128, 256])
    g.st_mc = din("st_mc", [DEPTH, 2, 4, 256, 256])
    g.st_mn = din("st_mn", [DEPTH, 2, 4, 256])
    g.st_mm = din("st_mm", [DEPTH, 2, 4])
    g.st_lru = din("st_lru", [DEPTH, 2, 1024])
    g.norm1_g = din("norm1_g", [DEPTH, DM])
    g.norm2_g = din("norm2_g", [DEPTH, DM])
    g.w_mod = din("w_mod", [DEPTH, DM, 6 * DM])
    g.b_mod = din("b_mod", [DEPTH, 6 * DM])
    g.w_in = din("w_in", [DEPTH, DM, D_IN])
    g.gla_w_alpha = din("gla_w_alpha", [DEPTH, 2, 16, 512])
    g.gla_b_alpha = din("gla_b_alpha", [DEPTH, 2, 512])
    g.gla_norm_g = din("gla_norm_g", [DEPTH, 1024])
    g.mlstm_b_if = din("mlstm_b_if", [DEPTH, 16])
    g.mlstm_norm_g = din("mlstm_norm_g", [DEPTH, 1024])
    g.lru_conv_w = din("lru_conv_w", [DEPTH, 4, 1024])
    g.lru_conv_b = din("lru_conv_b", [DEPTH, 1024])
    g.lru_gate_w = din("lru_gate_w", [DEPTH, 2, 2, 8, 128, 128])
    g.lru_gate_b = din("lru_gate_b", [DEPTH, 2, 2, 1024])
    g.lru_lambda = din("lru_lambda", [DEPTH, 2, 1024])
    g.w_branch = din("w_branch", [DEPTH, 3, 1024, DM])
    g.b_merge = din("b_merge", [DEPTH, 3 * DM])
    g.w_out = din("w_out", [DEPTH, DM, DM])
    g.ffn_w_up = din("ffn_w_up", [DEPTH, DM, 2 * D_FF])
    g.ffn_conv_w = din("ffn_conv_w", [DEPTH, 9, D_FF])
    g.ffn_conv_b = din("ffn_conv_b", [DEPTH, D_FF])
    g.ffn_w_down = din("ffn_w_down", [DEPTH, D_FF, DM])
    g.norm_f_g = din("norm_f_g", [DM])

    g.y = dout("y", [T, DM])
    g.o_gla = dout("o_gla", [NCTX, DEPTH, 2, 4, 128, 256])
    g.o_mc = dout("o_mc", [NCTX, DEPTH, 2, 4, 256, 256])
    g.o_mn = dout("o_mn", [NCTX, DEPTH, 2, 4, 256])
    g.o_mm = dout("o_mm", [NCTX, DEPTH, 2, 4])
    g.o_lru = dout("o_lru", [NCTX, DEPTH, 2, 1024])

    g.XT = scratch("XT", [DM, T], F32)
    g.XTv = g.XT.rearrange("(c p) t -> p c t", p=128)
    g.ZTa = scratch("ZTa", [C_MG, T], F32)
    g.ZTb = scratch("ZTb", [D_IN - C_MG, T], F32)
    g.ZT = ZRows(g.ZTa, g.ZTb)
    g.VT = scratch("VT", [T, 2048], BF16)
    g.OF = scratch("OF", [1024, T], F32)
    g.YT = scratch("YT", [3072, T], BF16)
    g.OFv = g.OF.rearrange("(c p) t -> p c t", p=128)
    g.YTv = g.YT.rearrange("(c p) t -> p c t", p=128)
    g.MT = scratch("MT", [DM, T], BF16)
    g.wbf = {name: scratch(name + "_bf", list(getattr(g, name).shape), BF16) for (name, _) in WCONV}
    g.MTv = g.MT.rearrange("(c p) t -> p c t", p=128)
    g.PT = scratch("PT", [D_FF, T], BF16)

    with ExitStack() as es:
        P = Prog(nc, es)
        cst = P.tile(es, "cst", [128, 512], F32)
        g.cst = cst
        P.dma("sp", cst[:], g.consts[:, :], w=[cst])
        g.identf = View(cst, cst[:, 0:128])
        g.onesf = View(cst, cst[:, 128:256])
        g.epsc = P.tile(es, "epsc", [128, 1], F32)
        P.dve(lambda e: e.memset(g.epsc[:], EPS), w=[g.epsc])
        g.lnqa = P.tile(es, "lnqa", [128, 1], F32)
        P.dve(lambda e: e.memset(g.lnqa[:], LN_QA), w=[g.lnqa])
        g.lnkb = P.tile(es, "lnkb", [128, 1], F32)
        P.dve(lambda e: e.memset(g.lnkb[:], LN_KB), w=[g.lnkb])
        g.mod = P.tile(es, "mod", [128, 96, 2], F32)
        g.sA1 = P.tile(es, "sA1", [128, 16, 2], F32)
        g.sA2 = P.tile(es, "sA2", [128, 16, 2], F32)
        g.n1g = P.tile(es, "n1g", [128, 16], F32)
        g.n2g = P.tile(es, "n2g", [128, 16], F32)
        g.nfg = P.tile(es, "nfg", [128, 16], F32)
        g.rs_ring = Ring([P.tile(es, f"rs{i}", [128, 512], F32) for i in range(2)])

        def run():
            phase_convert(P, g)
            P.barrier()
            for (name, _) in WCONV:
                setattr(g, name, g.wbf[name])
            phase_load_x(P, g)
            P.barrier()
            if STOP == "x":
                return
            for l in range(DEPTH):
                phase_mod(P, g, l)
                P.barrier()
                if STOP == "mod" and l == STOPL:
                    return
                phase_stage_a(P, g, l)
                P.barrier()
                if STOP == "a" and l == STOPL:
                    return
                if "a" not in SKIP:
                    phase_gla(P, g, l)
                    P.barrier()
                if STOP == "gla" and l == STOPL:
                    return
                if "b" not in SKIP:
                    phase_mlstm(P, g, l)
                    P.barrier()
                if STOP == "mlstm" and l == STOPL:
                    return
                if "c" not in SKIP:
                    phase_lru(P, g, l)
                    P.barrier()
                if STOP == "lru" and l == STOPL:
                    return
                phase_merge(P, g, l)
                P.barrier()
                if STOP == "merge" and l == STOPL:
                    return
                phase_out(P, g, l)
                P.barrier()
                if STOP == "out" and l == STOPL:
                    return
                phase_ffn_up(P, g, l)
                P.barrier()
                phase_ffn_down(P, g, l)
                P.barrier()
                if STOP == "ffn" and l == STOPL:
                    return
            phase_final(P, g)

        run()
        P.barrier()
        print("instructions:", P.nins)
    return nc


def make_consts():
    c = np.zeros((128, 512), np.float32)
    c[:, 0:128] = np.eye(128, dtype=np.float32)
    c[:, 128:256] = 1.0
    j = np.arange(128)[:, None]
    i = np.arange(128)[None, :]
    c[:, 256:384] = (j <= i)
    c[:, 384:512] = (j >= i)
    return c


def make_sel():
    s = np.zeros((16, 16, 128), np.float32)
    for r in range(16):
        s[r, r, :] = 1.0
    return s.reshape(16, 2048)


def kernel(**inp):
    f = lambda a: np.ascontiguousarray(np.asarray(a, dtype=np.float32))
    nc = build_program()
    consts = make_consts()
    shared = {k: f(inp[k]) for k in ("norm1_g", "norm2_g", "w_mod", "b_mod", "w_in", "gla_w_alpha", "gla_b_alpha",
                                     "gla_norm_g", "mlstm_norm_g", "lru_conv_w", "lru_conv_b", "lru_gate_w",
                                     "lru_gate_b", "lru_lambda", "w_branch", "b_merge", "w_out", "ffn_w_up",
                                     "ffn_conv_b", "ffn_w_down", "norm_f_g")}
    shared["mlstm_b_if"] = f(inp["mlstm_b_if"]).reshape(DEPTH, 16)
    shared["ffn_conv_w"] = f(inp["ffn_conv_w"]).reshape(DEPTH, 9, D_FF)
    shared["consts"] = consts
    shared["selc"] = make_sel()
    xs, xp = f(inp["x_sample"]), f(inp["x_prompt"])
    in_maps = []
    ncores = int(os.environ.get("MK_NCORES", NCORES))
    for c in range(ncores):
        m = dict(shared)
        m["xin"] = np.concatenate([xs[c], xp[4 * c:4 * c + 4].reshape(NCTX * CTXL, DM)], axis=0)
        m["cvec"] = np.stack([f(inp["c"])[c], f(inp["c_ctx"])], axis=0)
        m["st_gla"] = f(inp["state_gla"])[c]
        m["st_mc"] = f(inp["state_mlstm_c"])[c]
        m["st_mn"] = f(inp["state_mlstm_n"])[c]
        m["st_mm"] = f(inp["state_mlstm_m"])[c]
        m["st_lru"] = f(inp["state_rglru"])[c]
        in_maps.append(m)
    if os.environ.get("MK_TRACE"):
        res = run_bass_kernel_spmd(nc, in_maps, core_ids=list(range(ncores)), trace=True)
        print("EXEC_TIME_NS", res.exec_time_ns)
    else:
        res = run_bass_kernel_spmd(nc, in_maps, core_ids=list(range(ncores)))
    R = res.results
    if DUMP:
        return R
    y = np.stack([r["y"] for r in R], 0)
    y_sample = np.ascontiguousarray(y[:, :LAT, :])
    y_prompt = np.ascontiguousarray(y[:, LAT:, :]).reshape(NCORES * NCTX, CTXL, DM)
    cat = lambda k: np.concatenate([r[k] for r in R], axis=0)
    return (y_prompt, y_sample, cat("o_gla"), cat("o_mc"), cat("o_mn"), cat("o_mm"), cat("o_lru"))
```

```python
import os
import math
from contextlib import ExitStack
import numpy as np
import concourse.bass as bass
import concourse.mybir as mybir
from concourse.bass_utils import run_bass_kernel_spmd

F32 = mybir.dt.float32
BF16 = mybir.dt.bfloat16
ALU = mybir.AluOpType
AF = mybir.ActivationFunctionType
AX = mybir.AxisListType

NCORES = 8
DM = 2048
T = 5120
LAT = 4096
NCTX = 4
CTXL = 256
DEPTH = 2
D_IN = 15408
D_FF = 5632
EPS = 1e-6
NB = 256
LN_QA = math.log(128 ** -0.5)
LN_KB = math.log(256 ** -0.5)

C_QA, C_KA, C_VA, C_GA, C_RA, C_QB, C_KB, C_VB, C_OB, C_GB, C_XR, C_YR, C_MG = (
    0, 512, 1024, 2048, 3072, 3104, 4128, 5152, 6176, 7200, 7216, 8240, 9264)

STOP = os.environ.get("MK_STOP", "")
STOPL = int(os.environ.get("MK_STOPL", "0"))
SKIP = os.environ.get("MK_SKIP", "")
DUMP = [s for s in os.environ.get("MK_DUMP", "").split(",") if s]


class Buf:
    __slots__ = ("w", "r")

    def __init__(self):
        self.w = None
        self.r = {}


class Tl:
    def __init__(self, h):
        self.h = h
        self.buf = Buf()

    def __getitem__(self, k):
        return self.h[k]


def _b(x):
    return getattr(x, "buf", x)


class Prog:
    def __init__(self, nc, es):
        self.nc = nc
        self.eng = {"pe": nc.tensor, "dve": nc.vector, "act": nc.scalar, "sp": nc.sync}
        self.sem = {k: es.enter_context(nc.semaphore("s_" + k)) for k in ("pe", "dve", "act")}
        self.cnt = {"pe": 0, "dve": 0, "act": 0}
        self.known = {k: {} for k in self.eng}
        self.slots = {}
        for q, n in (("sp", 40), ("act", 40)):
            self.slots[q] = [[es.enter_context(nc.semaphore(f"d_{q}{i}")), 0, f"{q}{i}"] for i in range(n)]
        self.slot_i = {"sp": 0, "act": 0}
        self.pe_pending = False
        self.psum = es.enter_context(nc.psum_tensor("ps", [128, 4096], F32))
        self.pbuf = [Buf() for _ in range(8)]
        self.pi = 0
        self.nins = 0

    def _wait(self, e, tok):
        key, sem, val = tok
        if self.known[e].get(key, 0) >= val:
            return
        self.eng[e].wait_ge(sem, val)
        self.known[e][key] = val
        self.nins += 1

    def _deps(self, e, r, w):
        toks = []
        for b in r:
            if b.w is not None:
                toks.append(b.w)
        for b in w:
            if b.w is not None and b.w[0] != e:
                toks.append(b.w)
            toks.extend(b.r.values())
        for t in toks:
            if e == "pe" and t[0] == "pe":
                continue
            self._wait(e, t)

    def _mark(self, tok, r, w):
        for b in r:
            old = b.r.get(tok[0])
            if old is None or old[2] < tok[2]:
                b.r[tok[0]] = tok
        for b in w:
            b.w = tok
            b.r = {}

    def op(self, e, fn, r=(), w=(), ms=True):
        r = [_b(x) for x in r]
        w = [_b(x) for x in w]
        self._deps(e, r, w)
        ins = fn(self.eng[e])
        self.nins += 1
        if e == "pe":
            tok = ("pe", self.sem["pe"], self.cnt["pe"] + 1)
            if ms:
                ins.then_inc(self.sem["pe"], 1)
                self.cnt["pe"] += 1
                self.pe_pending = False
            else:
                self.pe_pending = True
        else:
            self.cnt[e] += 1
            ins.then_inc(self.sem[e], 1)
            tok = (e, self.sem[e], self.cnt[e])
        self._mark(tok, r, w)

    def pe(self, fn, r=(), w=(), ms=True):
        self.op("pe", fn, r, w, ms)

    def dve(self, fn, r=(), w=()):
        self.op("dve", fn, r, w)

    def act(self, fn, r=(), w=()):
        self.op("act", fn, r, w)

    def dma(self, q, out, in_, r=(), w=(), **kw):
        r = [_b(x) for x in r]
        w = [_b(x) for x in w]
        slots = self.slots[q]
        i = self.slot_i[q]
        self.slot_i[q] = (i + 1) % len(slots)
        sl = slots[i]
        if sl[1] > 0:
            self._wait(q, (sl[2], sl[0], sl[1]))
        self._deps(q, r, w)
        sl[1] += 16
        self.eng[q].dma_start(out=out, in_=in_, **kw).then_inc(sl[0], 16)
        self.nins += 1
        tok = (sl[2], sl[0], sl[1])
        self._mark(tok, r, w)

    def barrier(self):
        toks = [(k, self.sem[k], self.cnt[k]) for k in self.cnt if self.cnt[k] > 0]
        for q in self.slots:
            for sl in self.slots[q]:
                if sl[1] > 0:
                    toks.append((sl[2], sl[0], sl[1]))
        for e in self.eng:
            for t in toks:
                self._wait(e, t)

    def tile(self, es, name, shape, dt):
        self.uid = getattr(self, "uid", 0) + 1
        return Tl(es.enter_context(self.nc.sbuf_tensor(f"{name}_{self.uid}", list(shape), dt)))

    def ps(self, nb=1):
        if self.pi % nb:
            self.pi += nb - self.pi % nb
        if self.pi + nb > 8:
            self.pi = 0
        i = self.pi
        self.pi += nb
        return self.psum[:, i * 512:(i + nb) * 512], self.pbuf[i:i + nb]


class View:
    def __init__(self, t, ap):
        self.buf = t.buf
        self.ap = ap

    def __getitem__(self, k):
        return self.ap


class ZRows:
    def __init__(self, a, b):
        self.a = a
        self.b = b

    def __getitem__(self, k):
        rs, cs = k
        if rs.start >= C_MG:
            return self.b[rs.start - C_MG:rs.stop - C_MG, cs]
        assert rs.stop <= C_MG
        return self.a[rs.start:rs.stop, cs]


class Ring:
    def __init__(self, tiles):
        self.t = tiles
        self.i = 0

    def next(self):
        t = self.t[self.i]
        self.i = (self.i + 1) % len(self.t)
        return t


class D:
    pass


def seqs_in_span(s0, n):
    if s0 < LAT:
        return [(s0, n, ("lat", s0 == 0, s0 + n == LAT, -1))]
    out = []
    for i in range(NCTX):
        out.append((LAT + i * CTXL, CTXL, ("ctx", True, True, i)))
    return out


SPANS = [(0, 1024), (1024, 1024), (2048, 1024), (3072, 1024), (4096, 1024)]


def span_order(d):
    return SPANS if d == 0 else [SPANS[3], SPANS[2], SPANS[1], SPANS[0], SPANS[4]]


def chunk_info(s0, c, d):
    t0 = s0 + c * 128
    if t0 < LAT:
        first = (t0 == 0) if d == 0 else (t0 + 128 == LAT)
        last = (t0 + 128 == LAT) if d == 0 else (t0 == 0)
        return first, last, "lat", -1
    r = (t0 - LAT) % CTXL
    i = (t0 - LAT) // CTXL
    first = (r == 0) if d == 0 else (r + 128 == CTXL)
    last = (r + 128 == CTXL) if d == 0 else (r == 0)
    return first, last, "ctx", i


WCONV = [("w_mod", 2048), ("w_in", 512), ("lru_gate_w", 2048), ("w_branch", 2048), ("w_out", 2048), ("ffn_w_up", 2048), ("ffn_w_down", 2048)]


def flat2(ap, F):
    nd = len(ap.shape)
    names = " ".join(f"d{i}" for i in range(nd))
    fl = ap.rearrange(f"{names} -> ({names})")
    return fl.rearrange("(b p f) -> b p f", p=128, f=F)


def phase_convert(P, g):
    with ExitStack() as es:
        fr = Ring([P.tile(es, f"cf{i}", [128, 2048], F32) for i in range(4)])
        br = Ring([P.tile(es, f"cb{i}", [128, 2048], BF16) for i in range(4)])
        k = 0
        for (name, F) in WCONV:
            src = flat2(getattr(g, name), F)
            dst = flat2(g.wbf[name], F)
            for b in range(src.shape[0]):
                f_, b_ = fr.next(), br.next()
                P.dma("sp", f_[:, :F], src[b], w=[f_])
                if k % 3 == 2:
                    P.dve(lambda e: e.tensor_copy(b_[:, :F], f_[:, :F]), r=[f_], w=[b_])
                else:
                    P.act(lambda e: e.copy(b_[:, :F], f_[:, :F]), r=[f_], w=[b_])
                P.dma("act", dst[b], b_[:, :F], r=[b_])
                k += 1

def phase_load_x(P, g):
    with ExitStack() as es:
        xin = Ring([P.tile(es, f"xin{i}", [128, DM], F32) for i in range(2)])
        xo = Ring([P.tile(es, f"xo{i}", [128, 16, 512], F32) for i in range(2)])
        k = 0
        for blk in range(T // 512):
            o = xo.next()
            for tt in range(4):
                t0 = blk * 512 + tt * 128
                xi = xin.next()
                P.dma("sp", xi[:], g.xin[t0:t0 + 128, :], w=[xi])
                for q in range(4):
                    ps, pb = P.ps(1)
                    for j in range(4):
                        c = q * 4 + j
                        P.pe(lambda e: e.transpose(ps[:, j * 128:(j + 1) * 128], xi[:, c * 128:(c + 1) * 128], g.identf[:]),
                             r=[xi, g.identf], w=pb, ms=(j == 3))
                    src = ps.rearrange("p (j t) -> p j t", j=4)
                    dst = o[:, q * 4:q * 4 + 4, tt * 128:(tt + 1) * 128]
                    if k % 2 == 0:
                        P.act(lambda e: e.copy(dst, src), r=pb, w=[o])
                    else:
                        P.dve(lambda e: e.tensor_copy(dst, src), r=pb, w=[o])
                    k += 1
            P.dma("act", g.XTv[:, :, blk * 512:(blk + 1) * 512], o[:], r=[o])


def phase_mod(P, g, l):
    nc = P.nc
    with ExitStack() as es:
        ct = P.tile(es, "ct", [128, 16, 2], F32)
        cs = P.tile(es, "cs", [128, 16, 2], BF16)
        bm = P.tile(es, "bm", [128, 96], F32)
        wm = Ring([P.tile(es, f"wm{i}", [128, 16, 512], BF16) for i in range(3)])
        for j in range(2):
            P.dma("sp", ct[:, :, j], g.cvec[j].rearrange("(k p) -> p k", p=128), w=[ct], allow_slow_non_contiguous=True)
        P.dma("sp", bm[:], g.b_mod[l].rearrange("(c p) -> p c", p=128), w=[bm], allow_slow_non_contiguous=True)
        P.dma("sp", g.n1g[:], g.norm1_g[l].rearrange("(c p) -> p c", p=128), w=[g.n1g], allow_slow_non_contiguous=True)
        P.dma("sp", g.n2g[:], g.norm2_g[l].rearrange("(c p) -> p c", p=128), w=[g.n2g], allow_slow_non_contiguous=True)
        P.act(lambda e: e.activation(cs[:], ct[:], AF.Silu), r=[ct], w=[cs])
        wv = g.w_mod[l].rearrange("(k p) n -> p k n", p=128)
        for nb in range(24):
            w = wm.next()
            P.dma("sp", w[:], wv[:, :, nb * 512:(nb + 1) * 512], w=[w])
            for cc in range(4):
                ps, pb = P.ps(1)
                for kc in range(16):
                    P.pe(lambda e: e.matmul(ps[:, 0:2], w[:, kc, cc * 128:(cc + 1) * 128], cs[:, kc, :],
                                            start=(kc == 0), stop=(kc == 15)),
                         r=[w, cs], w=pb, ms=(kc == 15))
                col = nb * 4 + cc
                P.dve(lambda e: e.tensor_scalar(g.mod[:, col, :], ps[:, 0:2], bm[:, col:col + 1], None, ALU.add),
                      r=pb + [bm], w=[g.mod])
        P.dve(lambda e: e.tensor_scalar(g.sA1[:], g.mod[:, 16:32, :], 1.0, None, ALU.add), r=[g.mod], w=[g.sA1])
        P.dve(lambda e: e.tensor_scalar(g.sA2[:], g.mod[:, 64:80, :], 1.0, None, ALU.add), r=[g.mod], w=[g.sA2])
        for j in range(2):
            P.dve(lambda e: e.tensor_tensor(g.sA1[:, :, j], g.sA1[:, :, j], g.n1g[:], ALU.mult), r=[g.sA1, g.n1g], w=[g.sA1])
            P.dve(lambda e: e.tensor_tensor(g.sA2[:, :, j], g.sA2[:, :, j], g.n2g[:], ALU.mult), r=[g.sA2, g.n2g], w=[g.sA2])


def norm_block(P, g, xb, nt, scale_fn, shift_fn, out_fn, sq):
    P.act(lambda e: e.activation(sq[:, :, :nt], xb[:, :, :nt], AF.Square), r=[xb], w=[sq])
    ps, pb = P.ps(1)
    for c in range(16):
        P.pe(lambda e: e.matmul(ps[:, :nt], g.onesf[:], sq[:, c, :nt], start=(c == 0), stop=(c == 15)),
             r=[g.onesf, sq], w=pb, ms=(c == 15))
    rs = g.rs_ring.next()
    P.act(lambda e: e.activation(rs[:, :nt], ps[:, :nt], AF.Ln, bias=g.epsc[:, 0:1], scale=1.0 / DM), r=pb + [g.epsc], w=[rs])
    P.act(lambda e: e.activation(rs[:, :nt], rs[:, :nt], AF.Exp, scale=-0.5), r=[rs], w=[rs])
    P.dve(lambda e: e.tensor_tensor(xb[:, :, :nt], xb[:, :, :nt], rs[:, :nt].unsqueeze(1).broadcast_to([128, 16, nt]), ALU.mult),
          r=[xb, rs], w=[xb])
    for c in range(16):
        oap, ot = out_fn(c)
        sh = shift_fn(c)
        if sh is None:
            P.act(lambda e: e.activation(oap, xb[:, c, :nt], AF.Copy, scale=scale_fn(c)), r=[xb, g.mod, g.sA1, g.sA2, g.nfg], w=[ot])
        else:
            P.act(lambda e: e.activation(oap, xb[:, c, :nt], AF.Identity, bias=sh, scale=scale_fn(c)),
                  r=[xb, g.mod, g.sA1, g.sA2], w=[ot])


def build_uT(P, g, es_blk, uT, segs, which):
    sA = g.sA1 if which == 1 else g.sA2
    shb = 0 if which == 1 else 48
    loc = 0
    for (t0, n) in segs:
        o = 0
        while o < n:
            nt = min(NB, n - o)
            j = 0 if (t0 + o) < LAT else 1
            xb = g.xb_ring.next()
            P.dma("sp", xb[:, :, :nt], g.XTv[:, :, t0 + o:t0 + o + nt], w=[xb])
            lo = loc
            norm_block(P, g, xb, nt,
                       lambda c: sA[:, c, j:j + 1],
                       lambda c: g.mod[:, shb + c, j:j + 1],
                       lambda c: (uT[:, c, lo:lo + nt], uT),
                       g.sq)
            o += nt
            loc += nt


FSEG = ([(0, 512), (512, 1024), (2048, 2560), (2560, 3072), (3072, 3104), (3104, 3616), (3616, 4128),
         (4128, 4640), (4640, 5152), (6176, 6688), (6688, 7200), (7200, 7216), (7216, 7728), (7728, 8240),
         (8240, 8752), (8752, 9264)] + [(9264 + 512 * i, 9264 + 512 * (i + 1)) for i in range(12)])
VSEG = [(1024, 1536, 0), (1536, 2048, 512), (5152, 5664, 1024), (5664, 6176, 1536)]


def phase_stage_a(P, g, l):
    with ExitStack() as es:
        uT = P.tile(es, "uT", [128, 16, 2560], BF16)
        g.xb_ring = Ring([P.tile(es, f"xb{i}", [128, 16, NB], F32) for i in range(2)])
        g.sq = P.tile(es, "sq", [128, 16, NB], F32)
        wr = Ring([P.tile(es, f"wa{i}", [128, 16, 512], BF16) for i in range(2)])
        zs = Ring([P.tile(es, f"zs{i}", [128, 2560], F32) for i in range(2)])
        vs = Ring([P.tile(es, f"vs{i}", [128, 512], BF16) for i in range(3)])
        wv = g.w_in[l].rearrange("(k p) n -> p k n", p=128)
        k = 0
        for grp in range(2):
            g0 = grp * 2560
            build_uT(P, g, es, uT, [(g0, 2560)], 1)
            segs = [("F", a, b, 0) for (a, b) in FSEG] + [("V", a, b, vo) for (a, b, vo) in VSEG]
            for (kind, a, b, vo) in segs:
                wd = b - a
                w = wr.next()
                P.dma("sp", w[:, :, :wd], wv[:, :, a:b], w=[w])
                if kind == "F":
                    for cc in range((wd + 127) // 128):
                        wc = min(128, wd - cc * 128)
                        z = zs.next()
                        for tb in range(5):
                            ps, pb = P.ps(1)
                            for kc in range(16):
                                P.pe(lambda e: e.matmul(ps[:wc, :], w[:, kc, cc * 128:cc * 128 + wc], uT[:, kc, tb * 512:(tb + 1) * 512],
                                                        start=(kc == 0), stop=(kc == 15)),
                                     r=[w, uT], w=pb, ms=(kc == 15))
                            if k % 2 == 0:
                                P.act(lambda e: e.copy(z[:wc, tb * 512:(tb + 1) * 512], ps[:wc, :]), r=pb, w=[z])
                            else:
                                P.dve(lambda e: e.tensor_copy(z[:wc, tb * 512:(tb + 1) * 512], ps[:wc, :]), r=pb, w=[z])
                            k += 1
                        P.dma("act", g.ZT[a + cc * 128:a + cc * 128 + wc, g0:g0 + 2560], z[:wc, :], r=[z])
                else:
                    for tt in range(20):
                        ps, pb = P.ps(1)
                        for kc in range(16):
                            P.pe(lambda e: e.matmul(ps[:, :], uT[:, kc, tt * 128:(tt + 1) * 128], w[:, kc, :],
                                                    start=(kc == 0), stop=(kc == 15)),
                                 r=[w, uT], w=pb, ms=(kc == 15))
                        v = vs.next()
                        if k % 2 == 0:
                            P.act(lambda e: e.copy(v[:], ps[:, :]), r=pb, w=[v])
                        else:
                            P.dve(lambda e: e.tensor_copy(v[:], ps[:, :]), r=pb, w=[v])
                        k += 1
                        P.dma("act", g.VT[g0 + tt * 128:g0 + (tt + 1) * 128, vo:vo + 512], v[:], r=[v])


def mixer_epilogue(P, g, o, h, s0, gate_base, gate_func, ng, ybase, tl):
    gt, sqt, rs2, ybf = tl
    P.dma("sp", gt[:], g.ZT[gate_base + h * 256:gate_base + (h + 1) * 256, s0:s0 + 1024].rearrange("(v p) t -> p v t", p=128), w=[gt])
    P.act(lambda e: e.activation(sqt[:], o[:], AF.Square), r=[o], w=[sqt])
    ps, pb = P.ps(2)
    for half in range(2):
        for vc in range(2):
            P.pe(lambda e: e.matmul(ps[:, half * 512:(half + 1) * 512], g.onesf[:], sqt[:, vc, half * 512:(half + 1) * 512],
                                    start=(vc == 0), stop=(vc == 1)), r=[g.onesf, sqt], w=[pb[half]], ms=(vc == 1))
    P.act(lambda e: e.activation(rs2[:], ps, AF.Ln, bias=g.epsc[:, 0:1], scale=1.0 / 256), r=pb + [g.epsc], w=[rs2])
    P.act(lambda e: e.activation(rs2[:], rs2[:], AF.Exp, scale=-0.5), r=[rs2], w=[rs2])
    P.act(lambda e: e.activation(gt[:], gt[:], gate_func), r=[gt], w=[gt])
    for vc in range(2):
        P.dve(lambda e: e.tensor_tensor(o[:, vc, :], o[:, vc, :], rs2[:], ALU.mult), r=[o, rs2], w=[o])
        P.dve(lambda e: e.scalar_tensor_tensor(ybf[:, vc, :], o[:, vc, :], ng[:, h * 2 + vc:h * 2 + vc + 1], gt[:, vc, :],
                                               ALU.mult, ALU.mult), r=[o, ng, gt], w=[ybf])
    c0 = (ybase + h * 256) // 128
    P.dma("act", g.YTv[:, c0:c0 + 2, s0:s0 + 1024], ybf[:], r=[ybf])


def make_rmasks(P, es):
    rmF = P.tile(es, "rmF", [128, 1024], F32)
    rmB = P.tile(es, "rmB", [128, 1024], F32)
    P.dve(lambda e: e.memset(rmF[:], 1.0), w=[rmF])
    P.dve(lambda e: e.memset(rmF[:, 0::128], 0.0), r=[rmF], w=[rmF])
    P.dve(lambda e: e.memset(rmB[:], 1.0), w=[rmB])
    P.dve(lambda e: e.memset(rmB[:, 127::128], 0.0), r=[rmB], w=[rmB])
    return rmF, rmB


def phase_gla(P, g, l):
    with ExitStack() as es:
        t = lambda n, sh, dt: P.tile(es, n, sh, dt)
        wa = [t("gwa", [16, 512], F32) for _ in range(2)]
        nb = [t("gnb", [128, 4], F32) for _ in range(2)]
        ng = t("gng", [128, 8], F32)
        rm = make_rmasks(P, es)
        mask = [View(g.cst, g.cst[:, 256:384]), View(g.cst, g.cst[:, 384:512])]
        identb = t("identb", [128, 128], BF16)
        P.dve(lambda e: e.tensor_copy(identb[:], g.identf[:]), r=[g.identf], w=[identb])
        S = t("S", [128, 256], F32)
        Sbf = t("Sbf", [128, 256], BF16)
        ra = Ring([t("ra", [16, 1024], F32) for _ in range(2)])
        qr = Ring([t("q", [128, 1024], F32) for _ in range(2)])
        kr = Ring([t("k", [128, 1024], F32) for _ in range(2)])
        vr = Ring([t("v", [128, 8, 256], BF16) for _ in range(2)])
        sp = t("sp", [128, 1024], F32)
        Pc = t("Pc", [128, 1024], F32)
        tmp1 = t("tmp1", [128, 1024], F32)
        tmp2 = t("tmp2", [128, 1024], F32)
        qin = t("qin", [128, 1024], BF16)
        kin = t("kin", [128, 1024], BF16)
        kend = t("kend", [128, 1024], BF16)
        nbl = t("nbl", [128, 8], F32)
        dS = t("dS", [128, 8], F32)
        amr = Ring([t("am_all", [128, 8, 128], BF16) for _ in range(2)])
        ktr = Ring([t("kt_all", [128, 8, 128], BF16) for _ in range(2)])
        Ur = Ring([t("U_all", [128, 8, 256], F32) for _ in range(2)])
        Sbr = Ring([t("Sb_all", [128, 8, 256], BF16) for _ in range(2)])
        Sring = Ring([t("Sst", [128, 256], F32) for _ in range(6)])
        cur = [None]
        ofr = Ring([t("ofs", [128, 2, 1024], F32) for _ in range(2)])
        of0 = t("of0", [128, 2, 1024], F32)
        eptl = (t("gt", [128, 2, 1024], F32), t("sqt", [128, 2, 1024], F32), t("rs2", [128, 1024], F32),
                t("ybf", [128, 2, 1024], BF16))
        P.dma("sp", ng[:], g.gla_norm_g[l].rearrange("(c p) -> p c", p=128), w=[ng], allow_slow_non_contiguous=True)
        for d in range(2):
            P.dma("sp", wa[d][:], g.gla_w_alpha[l, d], w=[wa[d]])
            P.dma("sp", nb[d][:], g.gla_b_alpha[l, d].rearrange("(h p) -> p h", p=128), w=[nb[d]], allow_slow_non_contiguous=True)
            P.dve(lambda e: e.tensor_scalar(nb[d][:], nb[d][:], -1.0, None, ALU.mult), r=[nb[d]], w=[nb[d]])
        for d in range(2):
            lc = 127 if d == 0 else 0
            for h in range(4):
                for (s0, n) in span_order(d):
                    ra_t, qt, kt, vt = ra.next(), qr.next(), kr.next(), vr.next()
                    P.dma("sp", ra_t[:], g.ZT[C_RA + 16 * d:C_RA + 16 * d + 16, s0:s0 + 1024], w=[ra_t])
                    P.dma("sp", qt[:], g.ZT[C_QA + h * 128:C_QA + (h + 1) * 128, s0:s0 + 1024], w=[qt])
                    P.dma("sp", kt[:], g.ZT[C_KA + h * 128:C_KA + (h + 1) * 128, s0:s0 + 1024], w=[kt])
                    P.dma("sp", vt[:], g.VT[s0:s0 + 1024, h * 256:(h + 1) * 256].rearrange("(c p) v -> p c v", p=128), w=[vt])
                    if d == 1:
                        P.dma("sp", of0[:], g.OFv[:, h * 2:h * 2 + 2, s0:s0 + 1024], w=[of0])
                    ps, pb = P.ps(2)
                    for half in range(2):
                        P.pe(lambda e: e.matmul(ps[:, half * 512:(half + 1) * 512], wa[d][:, h * 128:(h + 1) * 128],
                                                ra_t[:, half * 512:(half + 1) * 512], start=True, stop=True),
                             r=[wa[d], ra_t], w=[pb[half]])
                    P.act(lambda e: e.activation(sp[:], ps, AF.Exp, bias=nb[d][:, h:h + 1], scale=-1.0), r=pb + [nb[d]], w=[sp])
                    P.act(lambda e: e.activation(sp[:], sp[:], AF.Ln, bias=1.0), r=[sp], w=[sp])
                    if d == 0:
                        P.dve(lambda e: e.tensor_tensor_scan(Pc[:], rm[0][:], sp[:], 0.0, ALU.mult, ALU.add), r=[rm[0], sp], w=[Pc])
                    else:
                        P.dve(lambda e: e.tensor_tensor_scan(Pc[:, ::-1], rm[1][:, ::-1], sp[:, ::-1], 0.0, ALU.mult, ALU.add),
                              r=[rm[1], sp], w=[Pc])
                    P.act(lambda e: e.activation(tmp1[:], Pc[:], AF.Exp, bias=g.lnqa[:, 0:1], scale=-1.0 / 16), r=[Pc, g.lnqa], w=[tmp1])
                    P.dve(lambda e: e.tensor_tensor(qin[:], qt[:], tmp1[:], ALU.mult), r=[qt, tmp1], w=[qin])
                    P.act(lambda e: e.activation(tmp2[:], Pc[:], AF.Exp, scale=1.0 / 16), r=[Pc], w=[tmp2])
                    P.dve(lambda e: e.tensor_tensor(kin[:], kt[:], tmp2[:], ALU.mult), r=[kt, tmp2], w=[kin])
                    P.dve(lambda e: e.tensor_scalar(nbl[:], Pc[:, lc::128], -1.0 / 16, None, ALU.mult), r=[Pc], w=[nbl])
                    P.act(lambda e: e.activation(dS[:], nbl[:], AF.Exp), r=[nbl], w=[dS])
                    for c in range(8):
                        P.act(lambda e: e.activation(tmp1[:, c * 128:(c + 1) * 128], Pc[:, c * 128:(c + 1) * 128], AF.Exp,
                                                     bias=nbl[:, c:c + 1], scale=1.0 / 16), r=[Pc, nbl], w=[tmp1])
                    P.dve(lambda e: e.tensor_tensor(kend[:], kt[:], tmp1[:], ALU.mult), r=[kt, tmp1], w=[kend])
                    ofs = ofr.next()
                    order = list(range(8)) if d == 0 else list(range(7, -1, -1))
                    am_all, kt_all, U_all, Sb_all = amr.next(), ktr.next(), Ur.next(), Sbr.next()
                    for c in order:
                        cs = slice(c * 128, (c + 1) * 128)
                        ps_a, pba = P.ps(1)
                        P.pe(lambda e: e.matmul(ps_a[:, 0:128], kin[:, cs], qin[:, cs], start=True, stop=True), r=[kin, qin], w=pba)
                        P.dve(lambda e: e.tensor_tensor(am_all[:, c, :], ps_a[:, 0:128], mask[d][:], ALU.mult), r=pba + [mask[d]], w=[am_all])
                    for c in order:
                        cs = slice(c * 128, (c + 1) * 128)
                        ps_t, pbt = P.ps(1)
                        pst = ps_t.bitcast(BF16)
                        P.pe(lambda e: e.transpose(pst[:, 0:128], kend[:, cs], identb[:]), r=[kend, identb], w=pbt)
                        P.act(lambda e: e.copy(kt_all[:, c, :], pst[:, 0:128]), r=pbt, w=[kt_all])
                    for i, c in enumerate(order):
                        ps_s, pbs = P.ps(1)
                        P.pe(lambda e: e.matmul(ps_s[:, 0:256], kt_all[:, c, :], vt[:, c, :], start=True, stop=True), r=[kt_all, vt], w=pbs)
                        if i % 2 == 0:
                            P.act(lambda e: e.copy(U_all[:, c, :], ps_s[:, 0:256]), r=pbs, w=[U_all])
                        else:
                            P.dve(lambda e: e.tensor_copy(U_all[:, c, :], ps_s[:, 0:256]), r=pbs, w=[U_all])
                    for c in order:
                        first, last, kind, ci = chunk_info(s0, c, d)
                        if first:
                            cur[0] = Sring.next()
                            S0 = cur[0]
                            if kind == "lat":
                                P.dma("sp", S0[:], g.st_gla[l, d, h], w=[S0])
                            else:
                                P.dve(lambda e: e.memset(S0[:], 0.0), w=[S0])
                        pre = cur[0]
                        P.act(lambda e: e.copy(Sb_all[:, c, :], pre[:]), r=[pre], w=[Sb_all])
                        nxt = Sring.next()
                        P.dve(lambda e: e.scalar_tensor_tensor(nxt[:], pre[:], dS[:, c:c + 1], U_all[:, c, :], ALU.mult, ALU.add),
                              r=[pre, dS, U_all], w=[nxt])
                        cur[0] = nxt
                        if last and kind == "ctx":
                            P.dma("act", g.o_gla[ci, l, d, h], nxt[:], r=[nxt])
                    for c in order:
                        cs = slice(c * 128, (c + 1) * 128)
                        ps_o, pbo = P.ps(1)
                        for vc in range(2):
                            vs_ = slice(vc * 128, (vc + 1) * 128)
                            P.pe(lambda e: e.matmul(ps_o[:, vs_], vt[:, c, vs_], am_all[:, c, :], start=True, stop=False), r=[vt, am_all], w=pbo, ms=False)
                            P.pe(lambda e: e.matmul(ps_o[:, vs_], Sb_all[:, c, vs_], qin[:, cs], start=False, stop=True), r=[Sb_all, qin], w=pbo, ms=True)
                        P.act(lambda e: e.copy(ofs[:, :, cs], ps_o[:, 0:256].rearrange("p (v t) -> p v t", v=2)), r=pbo, w=[ofs])
                    if d == 0:
                        P.dma("act", g.OFv[:, h * 2:h * 2 + 2, s0:s0 + 1024], ofs[:], r=[ofs])
                    else:
                        P.dve(lambda e: e.tensor_tensor(ofs[:], ofs[:], of0[:], ALU.add), r=[ofs, of0], w=[ofs])
                        mixer_epilogue(P, g, ofs, h, s0, C_GA, AF.Silu, ng, 0, eptl)
            if d == 0:
                P.barrier()


def rsl(a, b):
    return slice(b - 1, (a - 1) if a > 0 else None, -1)


def phase_mlstm(P, g, l):
    with ExitStack() as es:
        t = lambda n, sh, dt: P.tile(es, n, sh, dt)
        sel = t("sel", [16, 2048], F32)
        P.dma("sp", sel[:], g.selc[:, :], w=[sel])
        bif = t("bif", [128, 16], F32)
        nbif = t("nbif", [128, 16], F32)
        P.dma("sp", bif[:], g.mlstm_b_if[l].partition_broadcast(128), w=[bif])
        P.dve(lambda e: e.tensor_scalar(nbif[:], bif[:], -1.0, None, ALU.mult), r=[bif], w=[nbif])
        ng = t("mng", [128, 8], F32)
        P.dma("sp", ng[:], g.mlstm_norm_g[l].rearrange("(c p) -> p c", p=128), w=[ng], allow_slow_non_contiguous=True)
        rm = make_rmasks(P, es)
        mask = [View(g.cst, g.cst[:, 256:384]), View(g.cst, g.cst[:, 384:512])]
        identb = t("identb", [128, 128], BF16)
        P.dve(lambda e: e.tensor_copy(identb[:], g.identf[:]), r=[g.identf], w=[identb])
        onesb = t("onesb", [128, 128], BF16)
        P.dve(lambda e: e.tensor_copy(onesb[:], g.onesf[:]), r=[g.onesf], w=[onesb])
        zcol = t("zcol", [128, 1], F32)
        P.dve(lambda e: e.memset(zcol[:], 0.0), w=[zcol])
        mcr = Ring([t("mc", [128, 1], F32) for _ in range(2)])
        gbr = Ring([t("gb", [16, 1024], F32) for _ in range(1)])
        qr = Ring([t("q", [128, 2, 1024], F32) for _ in range(1)])
        kr = Ring([t("k", [128, 2, 1024], F32) for _ in range(1)])
        vr = Ring([t("v", [128, 8, 256], BF16) for _ in range(2)])
        ig, spf, lf, m, Pf, pm, pi = [t(n_, [128, 1024], F32) for n_ in ("ig", "spf", "lf", "m", "Pf", "pm", "pi")]
        eA, eB, efl, wp, kws = [t(n_, [128, 1024], F32) for n_ in ("eA", "eB", "efl", "wp", "kws")]
        qs, ks, qp, kw = [t(n_, [128, 2, 1024], BF16) for n_ in ("qs", "ks", "qp", "kw")]
        nbk = t("nbk", [128, 8], F32)
        ksum = t("ksum", [128, 2, 8], F32)
        smr = Ring([t("sm_all", [128, 8, 128], BF16) for _ in range(1)])
        dmx = Ring([t("dmx", [128, 128], F32) for _ in range(3)])
        kwtr = Ring([t("kwt_all", [128, 8, 2, 128], BF16) for _ in range(1)])
        Cbr = Ring([t("Cb_all", [128, 8, 512], BF16) for _ in range(1)])
        nbr = Ring([t("nb_all", [128, 8, 2, 128], BF16) for _ in range(1)])
        Ckr = Ring([t("Ck", [128, 4, 512], F32) for _ in range(1)])
        Cring = Ring([t("Cst", [128, 512], F32) for _ in range(6)])
        Nring = Ring([t("Nst", [128, 2], F32) for _ in range(6)])
        curC, curN = [None], [None]
        hsr = Ring([t("hs", [128, 2, 1024], F32) for _ in range(1)])
        of0 = t("of0", [128, 2, 1024], F32)
        eptl = (t("gt", [128, 2, 1024], F32), t("sqt", [128, 2, 1024], F32), t("rs2", [128, 1024], F32),
                t("ybf", [128, 2, 1024], BF16))
        for d in range(2):
            lc = 127 if d == 0 else 0
            for h in range(4):
                ri, rf = d * 8 + h, d * 8 + 4 + h
                mcar = None
                for (s0, n) in span_order(d):
                    gb, qt, kt, vt = gbr.next(), qr.next(), kr.next(), vr.next()
                    P.dma("sp", gb[:], g.ZT[C_GB:C_GB + 16, s0:s0 + 1024], w=[gb])
                    P.dma("sp", qt[:], g.ZT[C_QB + h * 256:C_QB + (h + 1) * 256, s0:s0 + 1024].rearrange("(c p) t -> p c t", p=128), w=[qt])
                    P.dma("sp", kt[:], g.ZT[C_KB + h * 256:C_KB + (h + 1) * 256, s0:s0 + 1024].rearrange("(c p) t -> p c t", p=128), w=[kt])
                    P.dma("sp", vt[:], g.VT[s0:s0 + 1024, 1024 + h * 256:1024 + (h + 1) * 256].rearrange("(c p) v -> p c v", p=128), w=[vt])
                    if d == 1:
                        P.dma("sp", of0[:], g.OFv[:, h * 2:h * 2 + 2, s0:s0 + 1024], w=[of0])
                    lat = s0 < LAT
                    if lat and mcar is None:
                        mcar = mcr.next()
                        P.dma("sp", mcar[:], g.st_mm[l, d, h:h + 1].partition_broadcast(128), w=[mcar])
                    ps, pb = P.ps(2)
                    for half in range(2):
                        P.pe(lambda e: e.matmul(ps[:, half * 512:(half + 1) * 512], sel[:, ri * 128:(ri + 1) * 128],
                                                gb[:, half * 512:(half + 1) * 512], start=True, stop=True), r=[sel, gb], w=[pb[half]])
                    P.act(lambda e: e.activation(ig[:], ps, AF.Identity, bias=bif[:, ri:ri + 1]), r=pb + [bif], w=[ig])
                    ps, pb = P.ps(2)
                    for half in range(2):
                        P.pe(lambda e: e.matmul(ps[:, half * 512:(half + 1) * 512], sel[:, rf * 128:(rf + 1) * 128],
                                                gb[:, half * 512:(half + 1) * 512], start=True, stop=True), r=[sel, gb], w=[pb[half]])
                    P.act(lambda e: e.activation(spf[:], ps, AF.Exp, bias=nbif[:, rf:rf + 1], scale=-1.0), r=pb + [nbif], w=[spf])
                    P.act(lambda e: e.activation(spf[:], spf[:], AF.Ln, bias=1.0), r=[spf], w=[spf])
                    P.dve(lambda e: e.tensor_scalar(lf[:], spf[:], -1.0, None, ALU.mult), r=[spf], w=[lf])
                    pieces = [(0, 1024)] if lat else [(i * CTXL, (i + 1) * CTXL) for i in range(NCTX)]
                    for (a, b) in pieces:
                        sl = slice(a, b) if d == 0 else rsl(a, b)
                        init = mcar[:, 0:1] if lat else 0.0
                        P.dve(lambda e: e.tensor_tensor_scan(m[:, sl], lf[:, sl], ig[:, sl], init, ALU.add, ALU.max),
                              r=[lf, ig] + ([mcar] if lat else []), w=[m])
                    mprev0 = mcar if lat else zcol
                    if lat:
                        mnext = mcr.next()
                        lastc = 1023 if d == 0 else 0
                        P.act(lambda e: e.copy(mnext[:], m[:, lastc:lastc + 1]), r=[m], w=[mnext])
                    if d == 0:
                        P.dve(lambda e: e.tensor_tensor_scan(Pf[:], rm[0][:], spf[:], 0.0, ALU.mult, ALU.add), r=[rm[0], spf], w=[Pf])
                    else:
                        P.dve(lambda e: e.tensor_tensor_scan(Pf[:, ::-1], rm[1][:, ::-1], spf[:, ::-1], 0.0, ALU.mult, ALU.add),
                              r=[rm[1], spf], w=[Pf])
                    P.dve(lambda e: e.tensor_tensor(pm[:], Pf[:], m[:], ALU.add), r=[Pf, m], w=[pm])
                    P.dve(lambda e: e.tensor_tensor(pi[:], Pf[:], ig[:], ALU.add), r=[Pf, ig], w=[pi])
                    P.act(lambda e: e.activation(eA[:], pm[:], AF.Exp, scale=-1.0), r=[pm], w=[eA])
                    P.act(lambda e: e.activation(eB[:], pi[:], AF.Exp, bias=g.lnkb[:, 0:1]), r=[pi, g.lnkb], w=[eB])
                    P.act(lambda e: e.activation(efl[:], m[:], AF.Exp, scale=-1.0), r=[m], w=[efl])
                    P.dve(lambda e: e.tensor_scalar(nbk[:], pm[:, lc::128], -1.0, LN_KB, ALU.mult, ALU.add), r=[pm], w=[nbk])
                    for c in range(8):
                        cs = slice(c * 128, (c + 1) * 128)
                        first, last, kind, ci = chunk_info(s0, c, d)
                        if first or (d == 0 and c == 0) or (d == 1 and c == 7):
                            mp, mpt = mprev0[:, 0:1], mprev0
                        else:
                            pc = (c - 1) * 128 + 127 if d == 0 else (c + 1) * 128
                            mp, mpt = m[:, pc:pc + 1], m
                        P.act(lambda e: e.activation(wp[:, cs], pm[:, cs], AF.Exp, bias=mp, scale=-1.0), r=[pm, mpt], w=[wp])
                        P.act(lambda e: e.activation(kws[:, cs], pi[:, cs], AF.Exp, bias=nbk[:, c:c + 1]), r=[pi, nbk], w=[kws])
                    for dc in range(2):
                        P.dve(lambda e: e.tensor_tensor(qs[:, dc, :], qt[:, dc, :], eA[:], ALU.mult), r=[qt, eA], w=[qs])
                        P.dve(lambda e: e.tensor_tensor(ks[:, dc, :], kt[:, dc, :], eB[:], ALU.mult), r=[kt, eB], w=[ks])
                        P.dve(lambda e: e.tensor_tensor(qp[:, dc, :], qt[:, dc, :], wp[:], ALU.mult), r=[qt, wp], w=[qp])
                        P.dve(lambda e: e.tensor_tensor(kw[:, dc, :], kt[:, dc, :], kws[:], ALU.mult), r=[kt, kws], w=[kw])
                    P.dve(lambda e: e.tensor_reduce(ksum[:], kw[:].rearrange("p d (c t) -> p d c t", t=128), AX.X, ALU.add), r=[kw], w=[ksum])
                    hs = hsr.next()
                    order = list(range(8)) if d == 0 else list(range(7, -1, -1))
                    sm_all, kwt_all, Cb_all, nb_all = smr.next(), kwtr.next(), Cbr.next(), nbr.next()
                    for c in order:
                        cs = slice(c * 128, (c + 1) * 128)
                        ps_s, pbs = P.ps(1)
                        for dc in range(2):
                            P.pe(lambda e: e.matmul(ps_s[:, 0:128], ks[:, dc, cs], qs[:, dc, cs], start=(dc == 0), stop=(dc == 1)),
                                 r=[ks, qs], w=pbs, ms=(dc == 1))
                        P.dve(lambda e: e.tensor_tensor(sm_all[:, c, :], ps_s[:, 0:128], mask[d][:], ALU.mult), r=pbs + [mask[d]], w=[sm_all])
                    for c in order:
                        cs = slice(c * 128, (c + 1) * 128)
                        ps_t, pbt = P.ps(1)
                        pst = ps_t.bitcast(BF16)
                        for dc in range(2):
                            P.pe(lambda e: e.transpose(pst[:, dc * 128:(dc + 1) * 128], kw[:, dc, cs], identb[:]), r=[kw, identb], w=pbt, ms=(dc == 1))
                        P.act(lambda e: e.copy(kwt_all[:, c, :, :], pst[:, 0:256].rearrange("p (d t) -> p d t", d=2)), r=pbt, w=[kwt_all])
                    for hf in range(2):
                        sub = order[hf * 4:(hf + 1) * 4]
                        Ck = Ckr.next()
                        for j, c in enumerate(sub):
                            ps_c, pbc = P.ps(1)
                            for dc in range(2):
                                P.pe(lambda e: e.matmul(ps_c[:, dc * 256:(dc + 1) * 256], kwt_all[:, c, dc, :], vt[:, c, :], start=True, stop=True),
                                     r=[kwt_all, vt], w=pbc, ms=(dc == 1))
                            if j % 2 == 0:
                                P.act(lambda e: e.copy(Ck[:, j, :], ps_c[:, 0:512]), r=pbc, w=[Ck])
                            else:
                                P.dve(lambda e: e.tensor_copy(Ck[:, j, :], ps_c[:, 0:512]), r=pbc, w=[Ck])
                        for j, c in enumerate(sub):
                            first, last, kind, ci = chunk_info(s0, c, d)
                            lcol = c * 128 + lc
                            if first:
                                curC[0], curN[0] = Cring.next(), Nring.next()
                                C0, N0 = curC[0], curN[0]
                                if kind == "lat":
                                    P.dma("sp", C0[:].rearrange("p (c v) -> p c v", c=2), g.st_mc[l, d, h].rearrange("(c p) v -> p c v", p=128), w=[C0])
                                    P.dma("sp", N0[:], g.st_mn[l, d, h].rearrange("(c p) -> p c", p=128), w=[N0], allow_slow_non_contiguous=True)
                                else:
                                    P.dve(lambda e: e.memset(C0[:], 0.0), w=[C0])
                                    P.dve(lambda e: e.memset(N0[:], 0.0), w=[N0])
                            preC, preN = curC[0], curN[0]
                            P.act(lambda e: e.copy(Cb_all[:, c, :], preC[:]), r=[preC], w=[Cb_all])
                            for dc in range(2):
                                P.act(lambda e: e.activation(nb_all[:, c, dc, :], g.onesf[:], AF.Copy, scale=preN[:, dc:dc + 1]), r=[g.onesf, preN], w=[nb_all])
                            nC, nN = Cring.next(), Nring.next()
                            dec = wp[:, lcol:lcol + 1]
                            P.dve(lambda e: e.scalar_tensor_tensor(nC[:], preC[:], dec, Ck[:, j, :], ALU.mult, ALU.add), r=[preC, wp, Ck], w=[nC])
                            P.dve(lambda e: e.scalar_tensor_tensor(nN[:], preN[:], dec, ksum[:, :, c], ALU.mult, ALU.add), r=[preN, wp, ksum], w=[nN])
                            curC[0], curN[0] = nC, nN
                            if last and kind == "ctx":
                                P.dma("act", g.o_mc[ci, l, d, h].rearrange("(c p) v -> p c v", p=128), nC[:].rearrange("p (c v) -> p c v", c=2), r=[nC])
                                P.dma("act", g.o_mn[ci, l, d, h].rearrange("(c p) -> p c", p=128), nN[:], r=[nN], allow_slow_non_contiguous=True)
                                P.dma("act", g.o_mm[ci, l, d, h:h + 1].rearrange("(a b) -> a b", a=1), m[0:1, lcol:lcol + 1], r=[m])
                    for c in order:
                        cs = slice(c * 128, (c + 1) * 128)
                        ps_d, pbd = P.ps(1)
                        P.pe(lambda e: e.matmul(ps_d[:, 0:128], onesb[:], sm_all[:, c, :], start=True, stop=False), r=[onesb, sm_all], w=pbd, ms=False)
                        for dc in range(2):
                            P.pe(lambda e: e.matmul(ps_d[:, 0:128], nb_all[:, c, dc, :], qp[:, dc, cs], start=False, stop=(dc == 1)),
                                 r=[nb_all, qp], w=pbd, ms=(dc == 1))
                        ps_n, pbn = P.ps(1)
                        for vc in range(2):
                            vs_ = slice(vc * 128, (vc + 1) * 128)
                            P.pe(lambda e: e.matmul(ps_n[:, vs_], vt[:, c, vs_], sm_all[:, c, :], start=True, stop=False), r=[vt, sm_all], w=pbn, ms=False)
                            for dc in range(2):
                                P.pe(lambda e: e.matmul(ps_n[:, vs_], Cb_all[:, c, dc * 256 + vc * 128:dc * 256 + (vc + 1) * 128], qp[:, dc, cs],
                                                        start=False, stop=(dc == 1)), r=[Cb_all, qp], w=pbn, ms=(dc == 1))
                        dm = dmx.next()
                        P.act(lambda e: e.activation(dm[:], ps_d[:, 0:128], AF.Abs), r=pbd, w=[dm])
                        P.dve(lambda e: e.tensor_tensor(dm[:], dm[:], efl[:, cs], ALU.max), r=[dm, efl], w=[dm])
                        P.dve(lambda e: e.reciprocal(dm[:], dm[:]), r=[dm], w=[dm])
                        for vc in range(2):
                            P.dve(lambda e: e.tensor_tensor(hs[:, vc, cs], ps_n[:, vc * 128:(vc + 1) * 128], dm[:], ALU.mult), r=pbn + [dm], w=[hs])
                    if lat:
                        mcar = mnext
                    if d == 0:
                        P.dma("act", g.OFv[:, h * 2:h * 2 + 2, s0:s0 + 1024], hs[:], r=[hs])
                    else:
                        P.dve(lambda e: e.tensor_tensor(hs[:], hs[:], of0[:], ALU.add), r=[hs, of0], w=[hs])
                        mixer_epilogue(P, g, hs, h, s0, C_OB, AF.Sigmoid, ng, 1024, eptl)
            if d == 0:
                P.barrier()


def phase_lru(P, g, l):
    with ExitStack() as es:
        t = lambda n, sh, dt: P.tile(es, n, sh, dt)
        cw = t("cw", [128, 8, 4], F32)
        cb = t("cb", [128, 8], F32)
        gbias = t("gbias", [128, 4, 8], F32)
        lam = t("lam", [128, 2, 8], F32)
        c1 = t("c1", [128, 2, 8], F32)
        c2 = t("c2", [128, 2, 8], F32)
        gw = t("gw", [128, 32, 128], BF16)
        for j in range(4):
            P.dma("sp", cw[:, :, j], g.lru_conv_w[l, j].rearrange("(c p) -> p c", p=128), w=[cw], allow_slow_non_contiguous=True)
        P.dma("sp", cb[:], g.lru_conv_b[l].rearrange("(c p) -> p c", p=128), w=[cb], allow_slow_non_contiguous=True)
        for d in range(2):
            for gi in range(2):
                P.dma("sp", gbias[:, d * 2 + gi, :], g.lru_gate_b[l, d, gi].rearrange("(c p) -> p c", p=128), w=[gbias], allow_slow_non_contiguous=True)
            P.dma("sp", lam[:, d, :], g.lru_lambda[l, d].rearrange("(c p) -> p c", p=128), w=[lam], allow_slow_non_contiguous=True)
        gwv = g.lru_gate_w[l].rearrange("d g n k j -> k (d g n) j")
        for q in range(4):
            P.dma("sp", gw[:, q * 8:(q + 1) * 8, :], gwv[:, q * 8:(q + 1) * 8, :], w=[gw])
        P.act(lambda e: e.activation(c1[:], lam[:], AF.Exp, scale=-1.0), r=[lam], w=[c1])
        P.act(lambda e: e.activation(c1[:], c1[:], AF.Ln, bias=1.0), r=[c1], w=[c1])
        P.dve(lambda e: e.tensor_scalar(c2[:], c1[:], -16.0, None, ALU.mult), r=[c1], w=[c2])
        P.dve(lambda e: e.tensor_scalar(c1[:], c1[:], -8.0, None, ALU.mult), r=[c1], w=[c1])
        x = t("x", [128, T], F32)
        xc = t("xc", [128, T], F32)
        xcb = t("xcb", [128, T], BF16)
        HS = t("HS", [128, T], F32)
        yr = t("yr", [128, T], F32)
        ybf = t("ybf", [128, T], BF16)
        r_, i_, a_, e2, tt, hd = [t(n_, [128, 1024], F32) for n_ in ("r_", "i_", "a_", "e2", "tt", "hd")]
        hcr = Ring([t("hc", [128, 1], F32) for _ in range(2)])
        seqs = [(0, LAT)] + [(LAT + i * CTXL, CTXL) for i in range(NCTX)]
        for cc in range(8):
            P.dma("sp", x[:], g.ZT[C_XR + cc * 128:C_XR + (cc + 1) * 128, 0:T], w=[x])
            P.dma("sp", yr[:], g.ZT[C_YR + cc * 128:C_YR + (cc + 1) * 128, 0:T], w=[yr])
            P.act(lambda e: e.activation(xc[:], x[:], AF.Identity, bias=cb[:, cc:cc + 1], scale=cw[:, cc, 2:3]), r=[x, cb, cw], w=[xc])
            for (t0, n) in seqs:
                for j in (0, 1, 3):
                    sh = j - 2
                    lo, hi = t0 + max(0, -sh), t0 + n - max(0, sh)
                    P.dve(lambda e: e.scalar_tensor_tensor(xc[:, lo:hi], x[:, lo + sh:hi + sh], cw[:, cc, j:j + 1], xc[:, lo:hi],
                                                           ALU.mult, ALU.add), r=[x, cw, xc], w=[xc])
            P.act(lambda e: e.copy(xcb[:], xc[:]), r=[xc], w=[xcb])
            for d in range(2):
                hcar = None
                for (s0, n) in span_order(d):
                    lat = s0 < LAT
                    ss = slice(s0, s0 + 1024)
                    pr, pbr = P.ps(2)
                    pi_, pbi = P.ps(2)
                    for gi, (pp, pbb) in enumerate(((pr, pbr), (pi_, pbi))):
                        for half in range(2):
                            P.pe(lambda e: e.matmul(pp[:, half * 512:(half + 1) * 512], gw[:, (d * 2 + gi) * 8 + cc, :],
                                                    xcb[:, s0 + half * 512:s0 + (half + 1) * 512], start=True, stop=True),
                                 r=[gw, xcb], w=[pbb[half]])
                    P.act(lambda e: e.activation(r_[:], pr, AF.Sigmoid, bias=gbias[:, d * 2, cc:cc + 1]), r=pbr + [gbias], w=[r_])
                    P.act(lambda e: e.activation(i_[:], pi_, AF.Sigmoid, bias=gbias[:, d * 2 + 1, cc:cc + 1]), r=pbi + [gbias], w=[i_])
                    P.act(lambda e: e.activation(a_[:], r_[:], AF.Exp, scale=c1[:, d, cc:cc + 1]), r=[r_, c1], w=[a_])
                    P.act(lambda e: e.activation(e2[:], r_[:], AF.Exp, scale=c2[:, d, cc:cc + 1]), r=[r_, c2], w=[e2])
                    P.act(lambda e: e.activation(e2[:], e2[:], AF.Sqrt, bias=1.0, scale=-1.0), r=[e2], w=[e2])
                    P.dve(lambda e: e.tensor_tensor(tt[:], i_[:], xc[:, ss], ALU.mult), r=[i_, xc], w=[tt])
                    P.dve(lambda e: e.tensor_tensor(tt[:], tt[:], e2[:], ALU.mult), r=[tt, e2], w=[tt])
                    if lat and hcar is None:
                        hcar = hcr.next()
                        P.dma("sp", hcar[:], g.st_lru[l, d, cc * 128:(cc + 1) * 128].rearrange("(p a) -> p a", a=1), w=[hcar],
                              allow_slow_non_contiguous=True)
                    pieces = [(0, 1024)] if lat else [(i * CTXL, (i + 1) * CTXL) for i in range(NCTX)]
                    for (a, b) in pieces:
                        sl = slice(a, b) if d == 0 else rsl(a, b)
                        init = hcar[:, 0:1] if lat else 0.0
                        P.dve(lambda e: e.tensor_tensor_scan(hd[:, sl], a_[:, sl], tt[:, sl], init, ALU.mult, ALU.add),
                              r=[a_, tt] + ([hcar] if lat else []), w=[hd])
                    if lat:
                        hn = hcr.next()
                        lastc = 1023 if d == 0 else 0
                        P.act(lambda e: e.copy(hn[:], hd[:, lastc:lastc + 1]), r=[hd], w=[hn])
                        hcar = hn
                    else:
                        for i in range(NCTX):
                            col = (i + 1) * CTXL - 1 if d == 0 else i * CTXL
                            P.dma("act", g.o_lru[i, l, d, cc * 128:(cc + 1) * 128].rearrange("(p a) -> p a", a=1), hd[:, col:col + 1], r=[hd],
                                  allow_slow_non_contiguous=True)
                    if d == 0:
                        P.act(lambda e: e.copy(HS[:, ss], hd[:]), r=[hd], w=[HS])
                    else:
                        P.dve(lambda e: e.tensor_tensor(HS[:, ss], HS[:, ss], hd[:], ALU.add), r=[HS, hd], w=[HS])
            P.act(lambda e: e.activation(x[:], yr[:], AF.Square), r=[yr], w=[x])
            P.dve(lambda e: e.tensor_scalar(x[:], x[:], 0.044715, 1.0, ALU.mult, ALU.add), r=[x], w=[x])
            P.dve(lambda e: e.tensor_tensor(x[:], x[:], yr[:], ALU.mult), r=[x, yr], w=[x])
            P.act(lambda e: e.activation(x[:], x[:], AF.Sigmoid, scale=1.5957691216057308), r=[x], w=[x])
            P.dve(lambda e: e.tensor_tensor(yr[:], yr[:], x[:], ALU.mult), r=[yr, x], w=[yr])
            P.dve(lambda e: e.tensor_tensor(ybf[:], HS[:], yr[:], ALU.mult), r=[HS, yr], w=[ybf])
            P.dma("act", g.YT[2048 + cc * 128:2048 + (cc + 1) * 128, 0:T], ybf[:], r=[ybf])


def phase_merge(P, g, l):
    BT = 256
    with ExitStack() as es:
        t = lambda n, sh, dt: P.tile(es, n, sh, dt)
        Wbr = t("Wbr", [128, 24, DM], BF16)
        bmg = t("bmg", [128, 48], F32)
        for n in range(3):
            for hf in range(2):
                P.dma("sp", Wbr[:, n * 8:(n + 1) * 8, hf * 1024:(hf + 1) * 1024],
                      g.w_branch[l, n].rearrange("(k p) c -> p k c", p=128)[:, :, hf * 1024:(hf + 1) * 1024], w=[Wbr])
        P.dma("sp", bmg[:], g.b_merge[l].rearrange("(c p) -> p c", p=128), w=[bmg], allow_slow_non_contiguous=True)
        Yr = Ring([t("Yb", [128, 24, BT], BF16) for _ in range(2)])
        Mr = Ring([t("Mb", [128, 16, BT], BF16) for _ in range(2)])
        mgr = Ring([t("mg", [128, BT], F32) for _ in range(6)])
        accr = Ring([t("acc", [128, BT], F32) for _ in range(2)])
        tr = Ring([t("tq", [128, BT], F32) for _ in range(2)])
        for blk in range(T // BT):
            t0 = blk * BT
            Y = Yr.next()
            P.dma("sp", Y[:], g.YTv[:, :, t0:t0 + BT], w=[Y])
            Mb = Mr.next()
            for colc in range(16):
                mgs = []
                for n in range(3):
                    mg = mgr.next()
                    r0 = C_MG + n * DM + colc * 128
                    P.dma("sp", mg[:], g.ZT[r0:r0 + 128, t0:t0 + BT], w=[mg])
                    P.act(lambda e: e.activation(mg[:], mg[:], AF.Sigmoid, bias=bmg[:, n * 16 + colc:n * 16 + colc + 1]), r=[mg, bmg], w=[mg])
                    mgs.append(mg)
                pss = []
                for n in range(3):
                    ps, pb = P.ps(1)
                    for kc in range(8):
                        P.pe(lambda e: e.matmul(ps[:, :BT], Wbr[:, n * 8 + kc, colc * 128:(colc + 1) * 128], Y[:, n * 8 + kc, :],
                                                start=(kc == 0), stop=(kc == 7)), r=[Wbr, Y], w=pb, ms=(kc == 7))
                    pss.append((ps, pb))
                acc, tq = accr.next(), tr.next()
                P.dve(lambda e: e.tensor_tensor(acc[:], pss[0][0][:, :BT], mgs[0][:], ALU.mult), r=pss[0][1] + [mgs[0]], w=[acc])
                P.dve(lambda e: e.tensor_tensor(tq[:], pss[1][0][:, :BT], mgs[1][:], ALU.mult), r=pss[1][1] + [mgs[1]], w=[tq])
                P.dve(lambda e: e.tensor_tensor(acc[:], acc[:], tq[:], ALU.add), r=[acc, tq], w=[acc])
                P.dve(lambda e: e.tensor_tensor(tq[:], pss[2][0][:, :BT], mgs[2][:], ALU.mult), r=pss[2][1] + [mgs[2]], w=[tq])
                P.dve(lambda e: e.tensor_tensor(Mb[:, colc, :], acc[:], tq[:], ALU.add), r=[acc, tq], w=[Mb])
            P.dma("act", g.MTv[:, :, t0:t0 + BT], Mb[:], r=[Mb])


def phase_out(P, g, l):
    with ExitStack() as es:
        t = lambda n, sh, dt: P.tile(es, n, sh, dt)
        Wo = t("Wo", [128, 16, DM], BF16)
        for hf in range(2):
            P.dma("sp", Wo[:, :, hf * 1024:(hf + 1) * 1024],
                  g.w_out[l].rearrange("(k p) c -> p k c", p=128)[:, :, hf * 1024:(hf + 1) * 1024], w=[Wo])
        Mr = Ring([t("Mb", [128, 16, 512], BF16) for _ in range(2)])
        Xr = Ring([t("xb", [128, 16, 512], F32) for _ in range(2)])
        for blk in range(T // 512):
            t0 = blk * 512
            j = 0 if t0 < LAT else 1
            Mb, xb = Mr.next(), Xr.next()
            P.dma("sp", Mb[:], g.MTv[:, :, t0:t0 + 512], w=[Mb])
            P.dma("sp", xb[:], g.XTv[:, :, t0:t0 + 512], w=[xb])
            for colc in range(16):
                ps, pb = P.ps(1)
                for kc in range(16):
                    P.pe(lambda e: e.matmul(ps[:, :], Wo[:, kc, colc * 128:(colc + 1) * 128], Mb[:, kc, :], start=(kc == 0), stop=(kc == 15)),
                         r=[Wo, Mb], w=pb, ms=(kc == 15))
                P.dve(lambda e: e.scalar_tensor_tensor(xb[:, colc, :], ps[:, :], g.mod[:, 32 + colc, j:j + 1], xb[:, colc, :], ALU.mult, ALU.add),
                      r=pb + [g.mod, xb], w=[xb])
            P.dma("act", g.XTv[:, :, t0:t0 + 512], xb[:], r=[xb])


FFN_GROUPS = [
    ([(0, 2112), (4096, 512)], (0, 32), 0, 4096),
    ([(1984, 2112), (4608, 512)], (1, 33), 2048, 4608),
]


def phase_ffn_up(P, g, l):
    NL = 2624
    with ExitStack() as es:
        t = lambda n, sh, dt: P.tile(es, n, sh, dt)
        vT = t("vT", [128, 16, NL], BF16)
        fcw = t("fcw", [128, 44, 9], F32)
        fcb = t("fcb", [128, 44], F32)
        for j in range(9):
            P.dma("sp", fcw[:, :, j], g.ffn_conv_w[l, j].rearrange("(c p) -> p c", p=128), w=[fcw], allow_slow_non_contiguous=True)
        P.dma("sp", fcb[:], g.ffn_conv_b[l].rearrange("(c p) -> p c", p=128), w=[fcb], allow_slow_non_contiguous=True)
        wv = g.ffn_w_up[l].rearrange("(k p) n -> p k n", p=128)
        for (segs, (r0, r1), lat_out, ctx_out) in FFN_GROUPS:
            with ExitStack() as es2:
                g.xb_ring = Ring([P.tile(es2, f"xb{i}", [128, 16, NB], F32) for i in range(2)])
                g.sq = P.tile(es2, "sq", [128, 16, NB], F32)
                build_uT(P, g, es2, vT, segs, 2)
                P.barrier()
            with ExitStack() as es2:
                t2 = lambda n, sh, dt: P.tile(es2, n, sh, dt)
                Wr = Ring([t2("Wu", [128, 16, 256], BF16) for _ in range(2)])
                hg, hu, acc = t2("hg", [128, NL], F32), t2("hu", [128, NL], F32), t2("acc", [128, NL], F32)
                pr = Ring([t2("pbf", [128, 2560], BF16) for _ in range(2)])
                tpr = Ring([t2("tp", [128, 2112], F32) for _ in range(2)])
                hg3 = hg[:, 0:2112].rearrange("p (r c) -> p r c", c=64)
                acc3 = acc[:, 0:2112].rearrange("p (r c) -> p r c", c=64)
                tbs = [(o, 512) for o in range(0, 2560, 512)] + [(2560, 64)]
                for cc in range(44):
                    W = Wr.next()
                    P.dma("sp", W[:, :, 0:128], wv[:, :, cc * 128:(cc + 1) * 128], w=[W])
                    P.dma("sp", W[:, :, 128:256], wv[:, :, D_FF + cc * 128:D_FF + (cc + 1) * 128], w=[W])
                    for (o, nt) in tbs:
                        psg, pbg = P.ps(1)
                        for kc in range(16):
                            P.pe(lambda e: e.matmul(psg[:, :nt], W[:, kc, 0:128], vT[:, kc, o:o + nt], start=(kc == 0), stop=(kc == 15)),
                                 r=[W, vT], w=pbg, ms=(kc == 15))
                        P.act(lambda e: e.copy(hg[:, o:o + nt], psg[:, :nt]), r=pbg, w=[hg])
                        psu, pbu = P.ps(1)
                        for kc in range(16):
                            P.pe(lambda e: e.matmul(psu[:, :nt], W[:, kc, 128:256], vT[:, kc, o:o + nt], start=(kc == 0), stop=(kc == 15)),
                                 r=[W, vT], w=pbu, ms=(kc == 15))
                        P.act(lambda e: e.copy(hu[:, o:o + nt], psu[:, :nt]), r=pbu, w=[hu])
                    la, lb = r0 * 64, r1 * 64
                    P.act(lambda e: e.activation(acc[:, la:lb], hg[:, la:lb], AF.Identity, bias=fcb[:, cc:cc + 1], scale=fcw[:, cc, 4:5]),
                          r=[hg, fcb, fcw], w=[acc])
                    for dy in (-1, 0, 1):
                        for dx in (-1, 0, 1):
                            if dy == 0 and dx == 0:
                                continue
                            ra, rb = max(r0, -dy), min(r1, 33 - dy)
                            ca, cb_ = max(0, -dx), min(64, 64 - dx)
                            wi = (dy + 1) * 3 + (dx + 1)
                            if wi % 2 == 1:
                                tp = tpr.next()
                                tp3 = tp[:, 0:2112].rearrange("p (r c) -> p r c", c=64)
                                P.act(lambda e: e.activation(tp3[:, ra:rb, ca:cb_], hg3[:, ra + dy:rb + dy, ca + dx:cb_ + dx], AF.Copy,
                                                             scale=fcw[:, cc, wi:wi + 1]), r=[hg, fcw], w=[tp])
                                P.dve(lambda e: e.tensor_tensor(acc3[:, ra:rb, ca:cb_], acc3[:, ra:rb, ca:cb_], tp3[:, ra:rb, ca:cb_], ALU.add),
                                      r=[acc, tp], w=[acc])
                                continue
                            P.dve(lambda e: e.scalar_tensor_tensor(acc3[:, ra:rb, ca:cb_], hg3[:, ra + dy:rb + dy, ca + dx:cb_ + dx],
                                                                   fcw[:, cc, wi:wi + 1], acc3[:, ra:rb, ca:cb_], ALU.mult, ALU.add),
                                  r=[hg, fcw, acc], w=[acc])
                    P.act(lambda e: e.activation(acc[:, 2112:NL], hg[:, 2112:NL], AF.Identity, bias=fcb[:, cc:cc + 1], scale=fcw[:, cc, 4:5]),
                          r=[hg, fcb, fcw], w=[acc])
                    for si in range(2):
                        q0 = 2112 + si * CTXL
                        for dx in (-1, 1):
                            lo, hi = q0 + max(0, -dx), q0 + CTXL - max(0, dx)
                            wi = 3 + (dx + 1)
                            P.dve(lambda e: e.scalar_tensor_tensor(acc[:, lo:hi], hg[:, lo + dx:hi + dx], fcw[:, cc, wi:wi + 1], acc[:, lo:hi],
                                                                   ALU.mult, ALU.add), r=[hg, fcw, acc], w=[acc])
                    pbf = pr.next()
                    P.act(lambda e: e.activation(acc[:, la:lb], acc[:, la:lb], AF.Silu), r=[acc], w=[acc])
                    P.act(lambda e: e.activation(acc[:, 2112:NL], acc[:, 2112:NL], AF.Silu), r=[acc], w=[acc])
                    P.dve(lambda e: e.tensor_tensor(pbf[:, 0:2048], acc[:, la:lb], hu[:, la:lb], ALU.mult), r=[acc, hu], w=[pbf])
                    P.dve(lambda e: e.tensor_tensor(pbf[:, 2048:2560], acc[:, 2112:NL], hu[:, 2112:NL], ALU.mult), r=[acc, hu], w=[pbf])
                    P.dma("act", g.PT[cc * 128:(cc + 1) * 128, lat_out:lat_out + 2048], pbf[:, 0:2048], r=[pbf])
                    P.dma("act", g.PT[cc * 128:(cc + 1) * 128, ctx_out:ctx_out + 512], pbf[:, 2048:2560], r=[pbf])
                P.barrier()


def phase_ffn_down(P, g, l):
    with ExitStack() as es:
        t = lambda n, sh, dt: P.tile(es, n, sh, dt)
        Pb = t("Pb", [128, 44, 1024], BF16)
        Wr = Ring([t("Wd", [128, 44, 256], BF16) for _ in range(2)])
        Xr = Ring([t("xd", [128, 1024], F32) for _ in range(3)])
        wv = g.ffn_w_down[l].rearrange("(k p) c -> p k c", p=128)
        PTv = g.PT.rearrange("(k p) t -> p k t", p=128)
        for blk in range(T // 1024):
            t0 = blk * 1024
            j = 0 if t0 < LAT else 1
            for q in range(4):
                P.dma("sp", Pb[:, q * 11:(q + 1) * 11, :], PTv[:, q * 11:(q + 1) * 11, t0:t0 + 1024], w=[Pb])
            for c2 in range(8):
                W = Wr.next()
                for q in range(4):
                    P.dma("sp", W[:, q * 11:(q + 1) * 11, :], wv[:, q * 11:(q + 1) * 11, c2 * 256:(c2 + 1) * 256], w=[W])
                for ci in range(2):
                    colc = c2 * 2 + ci
                    xb = Xr.next()
                    P.dma("sp", xb[:], g.XT[colc * 128:(colc + 1) * 128, t0:t0 + 1024], w=[xb])
                    for half in range(2):
                        ps, pb = P.ps(1)
                        for kc in range(44):
                            P.pe(lambda e: e.matmul(ps[:, :], W[:, kc, ci * 128:(ci + 1) * 128], Pb[:, kc, half * 512:(half + 1) * 512],
                                                    start=(kc == 0), stop=(kc == 43)), r=[W, Pb], w=pb, ms=(kc == 43))
                        hs_ = slice(half * 512, (half + 1) * 512)
                        P.dve(lambda e: e.scalar_tensor_tensor(xb[:, hs_], ps[:, :], g.mod[:, 80 + colc, j:j + 1], xb[:, hs_], ALU.mult, ALU.add),
                              r=pb + [g.mod, xb], w=[xb])
                    P.dma("act", g.XT[colc * 128:(colc + 1) * 128, t0:t0 + 1024], xb[:], r=[xb])


def phase_final(P, g):
    with ExitStack() as es:
        t = lambda n, sh, dt: P.tile(es, n, sh, dt)
        P.dma("sp", g.nfg[:], g.norm_f_g.rearrange("(c p) -> p c", p=128), w=[g.nfg], allow_slow_non_contiguous=True)
        g.xb_ring = Ring([t(f"xb{i}", [128, 16, NB], F32) for i in range(2)])
        g.sq = t("sq", [128, 16, NB], F32)
        ybr = Ring([t("yb", [128, 16, NB], F32) for _ in range(2)])
        yor = Ring([t("yo", [128, DM], F32) for _ in range(2)])
        k = 0
        for blk in range(T // NB):
            t0 = blk * NB
            xb = g.xb_ring.next()
            yb = ybr.next()
            P.dma("sp", xb[:], g.XTv[:, :, t0:t0 + NB], w=[xb])
            norm_block(P, g, xb, NB, lambda c: g.nfg[:, c:c + 1], lambda c: None, lambda c: (yb[:, c, :], yb), g.sq)
            for tt in range(NB // 128):
                yo = yor.next()
                for q in range(4):
                    ps, pb = P.ps(1)
                    for j in range(4):
                        c = q * 4 + j
                        P.pe(lambda e: e.transpose(ps[:, j * 128:(j + 1) * 128], yb[:, c, tt * 128:(tt + 1) * 128], g.identf[:]),
                             r=[yb, g.identf], w=pb, ms=(j == 3))
                    if k % 2 == 0:
                        P.act(lambda e: e.copy(yo[:, q * 512:(q + 1) * 512], ps[:, :]), r=pb, w=[yo])
                    else:
                        P.dve(lambda e: e.tensor_copy(yo[:, q * 512:(q + 1) * 512], ps[:, :]), r=pb, w=[yo])
                    k += 1
                P.dma("act", g.y[t0 + tt * 128:t0 + (tt + 1) * 128, :], yo[:], r=[yo])

def build_program():
    nc = bass.Bass("TRN2", target_bir_lowering=False)
    g = D()

    def din(name, shape):
        return nc.dram_tensor(name, list(shape), F32, kind="ExternalInput").ap()

    def dout(name, shape):
        return nc.dram_tensor(name, list(shape), F32, kind="ExternalOutput").ap()

    def scratch(name, shape, dt):
        kind = "ExternalOutput" if name in DUMP else "Internal"
        return nc.dram_tensor(name, list(shape), dt, kind=kind).ap()

    g.xin = din("xin", [T, DM])
    g.cvec = din("cvec", [2, DM])
    g.consts = din("consts", [128, 512])
    g.selc = din("selc", [16, 2048])
    g.st_gla = din("st_gla", [DEPTH, 2, 4, 128, 256])
    g.st_mc = din("st_mc", [DEPTH, 2, 4, 256, 256])
    g.st_mn = din("st_mn", [DEPTH, 2, 4, 256])
    g.st_mm = din("st_mm", [DEPTH, 2, 4])
    g.st_lru = din("st_lru", [DEPTH, 2, 1024])
    g.norm1_g = din("norm1_g", [DEPTH, DM])
    g.norm2_g = din("norm2_g", [DEPTH, DM])
    g.w_mod = din("w_mod", [DEPTH, DM, 6 * DM])
    g.b_mod = din("b_mod", [DEPTH, 6 * DM])
    g.w_in = din("w_in", [DEPTH, DM, D_IN])
    g.gla_w_alpha = din("gla_w_alpha", [DEPTH, 2, 16, 512])
    g.gla_b_alpha = din("gla_b_alpha", [DEPTH, 2, 512])
    g.gla_norm_g = din("gla_norm_g", [DEPTH, 1024])
    g.mlstm_b_if = din("mlstm_b_if", [DEPTH, 16])
    g.mlstm_norm_g = din("mlstm_norm_g", [DEPTH, 1024])
    g.lru_conv_w = din("lru_conv_w", [DEPTH, 4, 1024])
    g.lru_conv_b = din("lru_conv_b", [DEPTH, 1024])
    g.lru_gate_w = din("lru_gate_w", [DEPTH, 2, 2, 8, 128, 128])
    g.lru_gate_b = din("lru_gate_b", [DEPTH, 2, 2, 1024])
    g.lru_lambda = din("lru_lambda", [DEPTH, 2, 1024])
    g.w_branch = din("w_branch", [DEPTH, 3, 1024, DM])
    g.b_merge = din("b_merge", [DEPTH, 3 * DM])
    g.w_out = din("w_out", [DEPTH, DM, DM])
    g.ffn_w_up = din("ffn_w_up", [DEPTH, DM, 2 * D_FF])
    g.ffn_conv_w = din("ffn_conv_w", [DEPTH, 9, D_FF])
    g.ffn_conv_b = din("ffn_conv_b", [DEPTH, D_FF])
    g.ffn_w_down = din("ffn_w_down", [DEPTH, D_FF, DM])
    g.norm_f_g = din("norm_f_g", [DM])

    g.y = dout("y", [T, DM])
    g.o_gla = dout("o_gla", [NCTX, DEPTH, 2, 4, 128, 256])
    g.o_mc = dout("o_mc", [NCTX, DEPTH, 2, 4, 256, 256])
    g.o_mn = dout("o_mn", [NCTX, DEPTH, 2, 4, 256])
    g.o_mm = dout("o_mm", [NCTX, DEPTH, 2, 4])
    g.o_lru = dout("o_lru", [NCTX, DEPTH, 2, 1024])

    g.XT = scratch("XT", [DM, T], F32)
    g.XTv = g.XT.rearrange("(c p) t -> p c t", p=128)
    g.ZTa = scratch("ZTa", [C_MG, T], F32)
    g.ZTb = scratch("ZTb", [D_IN - C_MG, T], F32)
    g.ZT = ZRows(g.ZTa, g.ZTb)
    g.VT = scratch("VT", [T, 2048], BF16)
    g.OF = scratch("OF", [1024, T], F32)
    g.YT = scratch("YT", [3072, T], BF16)
    g.OFv = g.OF.rearrange("(c p) t -> p c t", p=128)
    g.YTv = g.YT.rearrange("(c p) t -> p c t", p=128)
    g.MT = scratch("MT", [DM, T], BF16)
    g.wbf = {name: scratch(name + "_bf", list(getattr(g, name).shape), BF16) for (name, _) in WCONV}
    g.MTv = g.MT.rearrange("(c p) t -> p c t", p=128)
    g.PT = scratch("PT", [D_FF, T], BF16)

    with ExitStack() as es:
        P = Prog(nc, es)
        cst = P.tile(es, "cst", [128, 512], F32)
        g.cst = cst
        P.dma("sp", cst[:], g.consts[:, :], w=[cst])
        g.identf = View(cst, cst[:, 0:128])
        g.onesf = View(cst, cst[:, 128:256])
        g.epsc = P.tile(es, "epsc", [128, 1], F32)
        P.dve(lambda e: e.memset(g.epsc[:], EPS), w=[g.epsc])
        g.lnqa = P.tile(es, "lnqa", [128, 1], F32)
        P.dve(lambda e: e.memset(g.lnqa[:], LN_QA), w=[g.lnqa])
        g.lnkb = P.tile(es, "lnkb", [128, 1], F32)
        P.dve(lambda e: e.memset(g.lnkb[:], LN_KB), w=[g.lnkb])
        g.mod = P.tile(es, "mod", [128, 96, 2], F32)
        g.sA1 = P.tile(es, "sA1", [128, 16, 2], F32)
        g.sA2 = P.tile(es, "sA2", [128, 16, 2], F32)
        g.n1g = P.tile(es, "n1g", [128, 16], F32)
        g.n2g = P.tile(es, "n2g", [128, 16], F32)
        g.nfg = P.tile(es, "nfg", [128, 16], F32)
        g.rs_ring = Ring([P.tile(es, f"rs{i}", [128, 512], F32) for i in range(2)])

        def run():
            phase_convert(P, g)
            P.barrier()
            for (name, _) in WCONV:
                setattr(g, name, g.wbf[name])
            phase_load_x(P, g)
            P.barrier()
            if STOP == "x":
                return
            for l in range(DEPTH):
                phase_mod(P, g, l)
                P.barrier()
                if STOP == "mod" and l == STOPL:
                    return
                phase_stage_a(P, g, l)
                P.barrier()
                if STOP == "a" and l == STOPL:
                    return
                if "a" not in SKIP:
                    phase_gla(P, g, l)
                    P.barrier()
                if STOP == "gla" and l == STOPL:
                    return
                if "b" not in SKIP:
                    phase_mlstm(P, g, l)
                    P.barrier()
                if STOP == "mlstm" and l == STOPL:
                    return
                if "c" not in SKIP:
                    phase_lru(P, g, l)
                    P.barrier()
                if STOP == "lru" and l == STOPL:
                    return
                phase_merge(P, g, l)
                P.barrier()
                if STOP == "merge" and l == STOPL:
                    return
                phase_out(P, g, l)
                P.barrier()
                if STOP == "out" and l == STOPL:
                    return
                phase_ffn_up(P, g, l)
                P.barrier()
                phase_ffn_down(P, g, l)
                P.barrier()
                if STOP == "ffn" and l == STOPL:
                    return
            phase_final(P, g)

        run()
        P.barrier()
        print("instructions:", P.nins)
    return nc


def make_consts():
    c = np.zeros((128, 512), np.float32)
    c[:, 0:128] = np.eye(128, dtype=np.float32)
    c[:, 128:256] = 1.0
    j = np.arange(128)[:, None]
    i = np.arange(128)[None, :]
    c[:, 256:384] = (j <= i)
    c[:, 384:512] = (j >= i)
    return c


def make_sel():
    s = np.zeros((16, 16, 128), np.float32)
    for r in range(16):
        s[r, r, :] = 1.0
    return s.reshape(16, 2048)


def kernel(**inp):
    f = lambda a: np.ascontiguousarray(np.asarray(a, dtype=np.float32))
    nc = build_program()
    consts = make_consts()
    shared = {k: f(inp[k]) for k in ("norm1_g", "norm2_g", "w_mod", "b_mod", "w_in", "gla_w_alpha", "gla_b_alpha",
                                     "gla_norm_g", "mlstm_norm_g", "lru_conv_w", "lru_conv_b", "lru_gate_w",
                                     "lru_gate_b", "lru_lambda", "w_branch", "b_merge", "w_out", "ffn_w_up",
                                     "ffn_conv_b", "ffn_w_down", "norm_f_g")}
    shared["mlstm_b_if"] = f(inp["mlstm_b_if"]).reshape(DEPTH, 16)
    shared["ffn_conv_w"] = f(inp["ffn_conv_w"]).reshape(DEPTH, 9, D_FF)
    shared["consts"] = consts
    shared["selc"] = make_sel()
    xs, xp = f(inp["x_sample"]), f(inp["x_prompt"])
    in_maps = []
    ncores = int(os.environ.get("MK_NCORES", NCORES))
    for c in range(ncores):
        m = dict(shared)
        m["xin"] = np.concatenate([xs[c], xp[4 * c:4 * c + 4].reshape(NCTX * CTXL, DM)], axis=0)
        m["cvec"] = np.stack([f(inp["c"])[c], f(inp["c_ctx"])], axis=0)
        m["st_gla"] = f(inp["state_gla"])[c]
        m["st_mc"] = f(inp["state_mlstm_c"])[c]
        m["st_mn"] = f(inp["state_mlstm_n"])[c]
        m["st_mm"] = f(inp["state_mlstm_m"])[c]
        m["st_lru"] = f(inp["state_rglru"])[c]
        in_maps.append(m)
    if os.environ.get("MK_TRACE"):
        res = run_bass_kernel_spmd(nc, in_maps, core_ids=list(range(ncores)), trace=True)
        print("EXEC_TIME_NS", res.exec_time_ns)
    else:
        res = run_bass_kernel_spmd(nc, in_maps, core_ids=list(range(ncores)))
    R = res.results
    if DUMP:
        return R
    y = np.stack([r["y"] for r in R], 0)
    y_sample = np.ascontiguousarray(y[:, :LAT, :])
    y_prompt = np.ascontiguousarray(y[:, LAT:, :]).reshape(NCORES * NCTX, CTXL, DM)
    cat = lambda k: np.concatenate([r[k] for r in R], axis=0)
    return (y_prompt, y_sample, cat("o_gla"), cat("o_mc"), cat("o_mn"), cat("o_mm"), cat("o_lru"))
```
